# Optimizing a Trainium2 kernel written in Bass

```python
import jax, jax.numpy as jnp
from jax import lax
import numpy as np

D_MODEL = 1024
BATCH = 1
SEQ = 16384
DEPTH = 2

GRID_W = 64
CTX_LEN = 256
HEAD_DIM = 64
N_Q_HEADS = 8
N_KV_HEADS = 2
GQA_GROUP = N_Q_HEADS // N_KV_HEADS
Q_W = N_Q_HEADS * HEAD_DIM
KV_W = N_KV_HEADS * HEAD_DIM
F_GROUPS = 8
F_GROUP_DIM = 64
F_W = F_GROUPS * F_GROUP_DIM
HYB_IN = Q_W + 2 * KV_W + F_W
HYB_OUT = Q_W + F_W
ROPE_HALF = HEAD_DIM // 2
ROPE_THETA = 10000.0
Q_BLOCK = 128
CONV_DIM = D_MODEL
CONV_WIDTH = 31
FFN_DIM = 2816
FFN_CONV_WIDTH = 3
N_EVEN = (DEPTH + 1) // 2
N_ODD = DEPTH // 2
NORM_EPS = 1e-6
LN_EPS = 1e-5

kernel_name = "hybrid_fourier_gqa_conformer_dit"


def rmsnorm(x, g):
    xf = x.astype(jnp.float32)
    y = xf * lax.rsqrt(jnp.mean(xf * xf, axis=-1, keepdims=True) + NORM_EPS)
    return (y * g.astype(jnp.float32)).astype(x.dtype)


def layernorm(x, g, b):
    xf = x.astype(jnp.float32)
    mu = jnp.mean(xf, axis=-1, keepdims=True)
    xc = xf - mu
    var = jnp.mean(xc * xc, axis=-1, keepdims=True)
    y = xc * lax.rsqrt(var + LN_EPS) * g.astype(jnp.float32) + b.astype(jnp.float32)
    return y.astype(x.dtype)


def modulate(h, shift, scale):
    return h * (1 + scale) + shift


def ada_params(cond, w_ada_i, b_ada_i):
    m = jax.nn.silu(cond) @ w_ada_i + b_ada_i
    m = m.reshape(m.shape[:-1] + (1, m.shape[-1]))
    if m.ndim == 2:
        m = m[None]
    return jnp.split(m, 6, axis=-1)


def dwconv(x, w, b):
    k = w.shape[0]
    pad = (k - 1) // 2
    y = lax.conv_general_dilated(x, w[:, None, :].astype(x.dtype), window_strides=(1,),
                                 padding=[(pad, pad)], dimension_numbers=("NWC", "WIO", "NWC"),
                                 feature_group_count=x.shape[-1])
    return y + b


def axial_rope(x):
    L = x.shape[1]
    rows = L // GRID_W
    row = jnp.repeat(jnp.arange(rows, dtype=jnp.float32), GRID_W)
    col = jnp.tile(jnp.arange(GRID_W, dtype=jnp.float32), rows)
    inv_freq = 1.0 / (ROPE_THETA ** (jnp.arange(0, ROPE_HALF, 2, dtype=jnp.float32) / ROPE_HALF))
    bshape = (1, L) + (1,) * (x.ndim - 3) + (ROPE_HALF,)

    def rot(xh, pos):
        ang = pos[:, None] * inv_freq[None, :]
        ang = jnp.concatenate([ang, ang], axis=-1).reshape(bshape)
        x1, x2 = jnp.split(xh, 2, axis=-1)
        return xh * jnp.cos(ang) + jnp.concatenate([-x2, x1], axis=-1) * jnp.sin(ang)

    xf = x.astype(jnp.float32)
    out = jnp.concatenate([rot(xf[..., :ROPE_HALF], row), rot(xf[..., ROPE_HALF:], col)], axis=-1)
    return out.astype(x.dtype)


def attend(qb, k, v):
    s = jnp.einsum("bqkgd,bskd->bkgqs", qb, k, preferred_element_type=jnp.float32) * (HEAD_DIM ** -0.5)
    p = jax.nn.softmax(s, axis=-1)
    return jnp.einsum("bkgqs,bskd->bqkgd", p.astype(v.dtype), v)


def blocked_attention(q, k, v):
    B, L = q.shape[0], q.shape[1]
    nb = L // Q_BLOCK
    qb = q.reshape(B, nb, Q_BLOCK, N_KV_HEADS, GQA_GROUP, HEAD_DIM).swapaxes(0, 1)
    o = lax.map(lambda t: attend(t, k, v), qb)
    return o.swapaxes(0, 1).reshape(B, L, Q_W)


def fourier_mix(f):
    B, L = f.shape[0], f.shape[1]
    g = f.reshape(B, L, F_GROUPS, F_GROUP_DIM).astype(jnp.float32)
    y = jnp.fft.fftn(g, axes=(1, 3), norm="ortho").real
    return y.reshape(B, L, F_W).astype(f.dtype)


def split_heads(u):
    B, L = u.shape[0], u.shape[1]
    q = u[..., :Q_W].reshape(B, L, N_KV_HEADS, GQA_GROUP, HEAD_DIM)
    k = u[..., Q_W:Q_W + KV_W].reshape(B, L, N_KV_HEADS, HEAD_DIM)
    v = u[..., Q_W + KV_W:Q_W + 2 * KV_W].reshape(B, L, N_KV_HEADS, HEAD_DIM)
    f = u[..., Q_W + 2 * KV_W:]
    return q, k, v, f


def hybrid_mixer(h_lat, h_ctx, w_in, q_gain, k_gain, w_out, need_ctx_out):
    B, L = h_lat.shape[0], h_lat.shape[1]
    q_l, k_l, v_l, f_l = split_heads(h_lat @ w_in)
    q_l = axial_rope(rmsnorm(q_l, q_gain))
    k_l = axial_rope(rmsnorm(k_l, k_gain))
    if need_ctx_out:
        q_c, k_c, v_c, f_c = split_heads(h_ctx @ w_in)
        q_c = rmsnorm(q_c, q_gain)
    else:
        kv_c = h_ctx @ w_in[:, Q_W:Q_W + 2 * KV_W]
        Lc = h_ctx.shape[1]
        k_c = kv_c[..., :KV_W].reshape(B, Lc, N_KV_HEADS, HEAD_DIM)
        v_c = kv_c[..., KV_W:].reshape(B, Lc, N_KV_HEADS, HEAD_DIM)
    k_c = rmsnorm(k_c, k_gain)
    k_all = jnp.concatenate([k_l, k_c], axis=1)
    v_all = jnp.concatenate([v_l, v_c], axis=1)
    a_l = blocked_attention(q_l, k_all, v_all)
    o_l = jnp.concatenate([a_l, fourier_mix(f_l)], axis=-1) @ w_out
    if need_ctx_out:
        a_c = attend(q_c, k_c, v_c).reshape(B, h_ctx.shape[1], Q_W)
        o_c = jnp.concatenate([a_c, fourier_mix(f_c)], axis=-1) @ w_out
        return o_l, o_c
    return o_l, None


def conformer_conv(h, w_pw1, b_pw1, w_dw, b_dw, ln_g, ln_b, w_pw2, b_pw2):
    u = h @ w_pw1 + b_pw1
    a, g = jnp.split(u, 2, axis=-1)
    u = a * jax.nn.sigmoid(g)
    u = dwconv(u, w_dw, b_dw)
    u = jax.nn.silu(layernorm(u, ln_g, ln_b))
    return u @ w_pw2 + b_pw2


def conv_ffn(h, w_up, w_dw, b_dw, w_down):
    u = dwconv(h @ w_up, w_dw, b_dw)
    a, b = jnp.split(u, 2, axis=-1)
    return (jax.nn.silu(a) * b) @ w_down


def setup_inputs(seed: int = 0) -> dict:
    key = jax.random.key(seed)
    ks = jax.random.split(key, 32)
    D = D_MODEL
    nrm = lambda k, shape, s: jax.random.normal(k, shape, jnp.float32) * s
    return {
        "x": nrm(ks[0], (BATCH, SEQ, D), 1.0),
        "c": nrm(ks[1], (BATCH, D), 1.0),
        "ctx": nrm(ks[2], (BATCH, CTX_LEN, D), 1.0),
        "c_ctx": nrm(ks[3], (D,), 1.0),
        "w_ada": nrm(ks[4], (DEPTH, D, 6 * D), D ** -0.5),
        "b_ada": nrm(ks[5], (DEPTH, 6 * D), 0.02),
        "g_mix": 1.0 + nrm(ks[6], (DEPTH, D), 0.02),
        "g_ffn": 1.0 + nrm(ks[7], (DEPTH, D), 0.02),
        "w_in_hyb": nrm(ks[8], (N_EVEN, D, HYB_IN), D ** -0.5),
        "q_gain": 1.0 + nrm(ks[9], (N_EVEN, HEAD_DIM), 0.02),
        "k_gain": 1.0 + nrm(ks[10], (N_EVEN, HEAD_DIM), 0.02),
        "w_out_hyb": nrm(ks[11], (N_EVEN, HYB_OUT, D), HYB_OUT ** -0.5),
        "w_pw1": nrm(ks[12], (N_ODD, D, 2 * CONV_DIM), D ** -0.5),
        "b_pw1": nrm(ks[13], (N_ODD, 2 * CONV_DIM), 0.02),
        "w_cdw": nrm(ks[14], (N_ODD, CONV_WIDTH, CONV_DIM), CONV_WIDTH ** -0.5),
        "b_cdw": nrm(ks[15], (N_ODD, CONV_DIM), 0.02),
        "ln_g": 1.0 + nrm(ks[16], (N_ODD, CONV_DIM), 0.02),
        "ln_b": nrm(ks[17], (N_ODD, CONV_DIM), 0.02),
        "w_pw2": nrm(ks[18], (N_ODD, CONV_DIM, D), CONV_DIM ** -0.5),
        "b_pw2": nrm(ks[19], (N_ODD, D), 0.02),
        "w_up": nrm(ks[20], (DEPTH, D, 2 * FFN_DIM), D ** -0.5),
        "w_fdw": nrm(ks[21], (DEPTH, FFN_CONV_WIDTH, 2 * FFN_DIM), FFN_CONV_WIDTH ** -0.5),
        "b_fdw": nrm(ks[22], (DEPTH, 2 * FFN_DIM), 0.02),
        "w_down": nrm(ks[23], (DEPTH, FFN_DIM, D), FFN_DIM ** -0.5),
    }


def reference(x, c, ctx, c_ctx, w_ada, b_ada, g_mix, g_ffn, w_in_hyb, q_gain, k_gain, w_out_hyb,
              w_pw1, b_pw1, w_cdw, b_cdw, ln_g, ln_b, w_pw2, b_pw2, w_up, w_fdw, b_fdw, w_down):
    x_lat = x
    x_ctx = ctx
    for i in range(DEPTH):
        j = i // 2
        is_even = (i % 2 == 0)
        ctx_next = any(l % 2 == 0 for l in range(i + 1, DEPTH))
        ctx_here = is_even or ctx_next
        sh1, sc1, g1, sh2, sc2, g2 = ada_params(c, w_ada[i], b_ada[i])
        csh1, csc1, cg1, csh2, csc2, cg2 = ada_params(c_ctx, w_ada[i], b_ada[i])

        h_lat = modulate(rmsnorm(x_lat, g_mix[i]), sh1, sc1)
        h_ctx = modulate(rmsnorm(x_ctx, g_mix[i]), csh1, csc1) if ctx_here else None
        if is_even:
            o_lat, o_ctx = hybrid_mixer(h_lat, h_ctx, w_in_hyb[j], q_gain[j], k_gain[j], w_out_hyb[j], ctx_next)
        else:
            o_lat = conformer_conv(h_lat, w_pw1[j], b_pw1[j], w_cdw[j], b_cdw[j], ln_g[j], ln_b[j], w_pw2[j], b_pw2[j])
            o_ctx = conformer_conv(h_ctx, w_pw1[j], b_pw1[j], w_cdw[j], b_cdw[j], ln_g[j], ln_b[j], w_pw2[j], b_pw2[j]) if ctx_next else None
        x_lat = x_lat + g1 * o_lat
        h = modulate(rmsnorm(x_lat, g_ffn[i]), sh2, sc2)
        x_lat = x_lat + g2 * conv_ffn(h, w_up[i], w_fdw[i], b_fdw[i], w_down[i])
        if ctx_next:
            x_ctx = x_ctx + cg1 * o_ctx
            hc = modulate(rmsnorm(x_ctx, g_ffn[i]), csh2, csc2)
            x_ctx = x_ctx + cg2 * conv_ffn(hc, w_up[i], w_fdw[i], b_fdw[i], w_down[i])
    return x_lat
```

```python
import numpy as np
import ml_dtypes
from contextlib import ExitStack
import concourse.bass as bass
import concourse.mybir as mybir
from concourse.bass_utils import run_bass_kernel_spmd

F32 = mybir.dt.float32
BF16 = mybir.dt.bfloat16
ALU = mybir.AluOpType
AF = mybir.ActivationFunctionType

NCORE = 8
D = 1024
SEQ = 16384
OWN = 2048
HALO = 17
NT = OWN + 2 * HALO
NTP = 17 * 128
CTX = 256
TT = [(0, 416), (416, 832), (832, 1248), (1248, 1664), (1664, 2082)]
FFN = 2816
NJ = FFN // 128
EPS = 1e-6
LN_EPS = 1e-5
NKC = 130
_LAST = {}
ENGS = ["tensor", "vector", "scalar", "gpsimd", "sync"]

V_GMIX = 0; V_GFFN = 16; V_BPW1 = 32; V_BCDW = 48; V_LNG = 56; V_LNB = 64; V_BPW2 = 72; V_WCDW = 80
V_BADA = V_WCDW + 31 * 8
V_BFDW = V_BADA + 96
V_WFDW = V_BFDW + 88
V_QK = V_WFDW + 264
NVEC = V_QK + 2


class Tile:
    __slots__ = ("ap", "w", "r")

    def __init__(self, ap):
        self.ap = ap
        self.w = None
        self.r = {}


class KB:
    def __init__(self, nc, stack):
        self.nc = nc
        self.stack = stack
        self.prog = {e: [] for e in ENGS}
        self.semobj = {}
        self.cur = {}
        self.known = {e: {} for e in ENGS}
        self.pending = {e: ([], []) for e in ENGS}
        self.nsem = 0
        for e in ENGS:
            self.cur[e] = [self.new_sem("e_" + e), 0]
        self.dmapool = {q: [[self.new_sem("d_%s%d" % (q, k)), 0] for k in range(n)]
                        for q, n in (("sync", 12), ("gpsimd", 8), ("scalar", 8))}
        self.dmaidx = {"sync": 0, "gpsimd": 0, "scalar": 0}
        self.done_sems = {}

    def new_sem(self, name):
        name = "%s_%d" % (name, self.nsem)
        self.nsem += 1
        self.semobj[name] = self.stack.enter_context(self.nc.semaphore(name))
        return name

    def _waits(self, eng, reads, writes, extra=()):
        need = {}

        def add(s, v):
            if need.get(s, 0) < v:
                need[s] = v
        for t in reads:
            if t.w is not None:
                add(*t.w)
        for t in writes:
            if t.w is not None:
                add(*t.w)
            for s, v in t.r.items():
                add(s, v)
        for s, v in extra:
            add(s, v)
        kn = self.known[eng]
        own = self.cur[eng][0]
        waits = []
        for s, v in need.items():
            if kn.get(s, 0) >= v:
                continue
            if eng == "tensor" and s == own:
                continue
            kn[s] = v
            waits.append((s, v))
        return waits

    def _register(self, ev, reads, writes):
        s, v = ev
        for t in reads:
            if t.r.get(s, 0) < v:
                t.r[s] = v
        for t in writes:
            t.w = ev
            t.r = {}

    def op(self, eng, fn, reads=(), writes=(), sig=True):
        waits = self._waits(eng, reads, writes)
        pr, pw = self.pending[eng]
        if sig:
            c = self.cur[eng]
            if c[1] >= 30000:
                self.done_sems[c[0]] = c[1]
                c[0] = self.new_sem("e_" + eng)
                c[1] = 0
            c[1] += 1
            ev = (c[0], c[1])
            self._register(ev, list(reads) + pr, list(writes) + pw)
            self.pending[eng] = ([], [])
            self.prog[eng].append((waits, fn, ev, 1))
        else:
            pr.extend(reads)
            pw.extend(writes)
            self.prog[eng].append((waits, fn, None, 0))

    def dma(self, q, out_ap, in_ap, reads=(), writes=()):
        pool = self.dmapool[q]
        k = self.dmaidx[q] % len(pool)
        self.dmaidx[q] += 1
        sname, cnt = pool[k]
        extra = [(sname, cnt)] if cnt > 0 else []
        waits = self._waits(q, reads, writes, extra)
        pool[k][1] = cnt + 16
        ev = (sname, cnt + 16)
        self._register(ev, reads, writes)
        self.prog[q].append((waits, lambda e: e.dma_start(out=out_ap, in_=in_ap), ev, 16))

    def allgather(self, in_t, out_t, in_handle, out_handle):
        s = self.new_sem("cc")
        waits = self._waits("gpsimd", [in_t], [out_t])
        ev = (s, 1)
        self._register(ev, [in_t], [out_t])

        def fn(e):
            return e.collective_compute(
                "AllGather", ALU.bypass, replica_groups=[list(range(NCORE))],
                ins=[in_handle.ap().opt()], outs=[out_handle.ap().opt()])
        self.prog["gpsimd"].append((waits, fn, ev, None))

    def barrier(self):
        evs = dict(self.done_sems)
        for e in ENGS:
            c = self.cur[e]
            if c[1] > 0:
                evs[c[0]] = c[1]
        for q, pool in self.dmapool.items():
            for s, c in pool:
                if c > 0:
                    evs[s] = c
        for e in ENGS:
            kn = self.known[e]
            waits = []
            for s, v in evs.items():
                if kn.get(s, 0) < v:
                    kn[s] = v
                    waits.append((s, v))
            self.prog[e].append((waits, None, None, 0))

    def emit_all(self, final_waits):
        kn = self.known["sync"]
        waits = [(s, v) for s, v in final_waits if kn.get(s, 0) < v]
        self.prog["sync"].append((waits, None, None, 0))
        with self.nc.Block() as block:
            for eng in ENGS:
                getattr(block, eng)(self._runner(eng))

    def _runner(self, eng):
        prog = self.prog[eng]
        semobj = self.semobj

        def run(e):
            for waits, fn, ev, inc in prog:
                for s, v in waits:
                    e.wait_ge(semobj[s], v)
                if fn is None:
                    continue
                ins = fn(e)
                if ev is not None:
                    if inc is None:
                        ins.then_inc(semobj[ev[0]])
                    else:
                        ins.then_inc(semobj[ev[0]], inc)
        return run


class Mem:
    def __init__(self, big, base, limit, kb):
        self.big = big
        self.base = base
        self.top = base
        self.limit = limit
        self.hi = limit
        self.kb = kb

    def mark(self):
        return self.top

    def release(self, m):
        self.top = m
        self.kb.barrier()

    def f32(self, cols, parts=128):
        off = self.top
        self.top += cols
        assert self.top <= self.hi, ("SBUF overflow", self.top, self.hi)
        return self.big[0:parts, off:off + cols]

    def bf16(self, cols, parts=128):
        c32 = (cols + 1) // 2
        off = self.top
        self.top += c32
        assert self.top <= self.hi, ("SBUF overflow", self.top, self.hi)
        return self.big[0:parts, off:off + c32].bitcast(BF16)[:, 0:cols]

    def bf16_top(self, cols, parts=128):
        c32 = (cols + 1) // 2
        self.hi -= c32
        assert self.top <= self.hi, ("SBUF overflow", self.top, self.hi)
        return self.big[0:parts, self.hi:self.hi + c32].bitcast(BF16)[:, 0:cols]

    def release_top(self):
        self.hi = self.limit
        self.kb.barrier()


def build_program(debug=None):
    nc = bass.Bass("TRN2", target_bir_lowering=False)
    stack = ExitStack()
    with stack:
        _build(nc, stack, debug)
    return nc


def _build(nc, stack, debug):
    kb = KB(nc, stack)
    _LAST['kb'] = kb

    def din(name, shape, dt=F32):
        return nc.dram_tensor(name, list(shape), dt, kind="ExternalInput")

    def dtmp(name, shape, dt):
        return nc.dram_tensor(name, list(shape), dt)

    x_d = din("x", [NTP, D])
    ctx_d = din("ctx", [CTX, D])
    cT_d = din("cT", [128, 16])
    wada_d = din("w_ada", [2, D, 768])
    vecs_d = din("vecs", [128, NVEC])
    consts_d = din("consts", [128, 5 * 128])
    rope_d = din("rope", [128, 2 * NT])
    masks_d = din("masks", [128, 2])
    win_d = din("w_in", [D, 1280])
    wf_d = din("wf", [D, 64])
    wout_d = din("w_out", [D, D])
    wpw1_d = din("w_pw1", [D, 2 * D])
    wpw2_d = din("w_pw2", [D, D])
    wup_d = din("w_up", [2, D, 2 * FFN])
    wdown_d = din("w_down", [2, FFN, D])
    e64_d = din("e64", [64, 128], BF16)
    cs12_d = din("cs12", [128, 512], BF16)
    tw_d = din("tw", [128, 1024])
    cs3_d = din("cs3", [128, 2 * 18], BF16)
    out_d = nc.dram_tensor("out", [OWN, D], F32, kind="ExternalOutput")

    k_in = dtmp("k_in", [128, OWN], BF16)
    k_all = dtmp("k_all", [NCORE * 128, OWN], BF16)
    v_in = dtmp("v_in", [128, 16 * 130], BF16)
    v_all = dtmp("v_all", [NCORE * 128, 16 * 130], BF16)
    h_in = dtmp("h_in", [D, OWN], BF16)
    h_all = dtmp("h_all", [NCORE * D, OWN], BF16)
    tp_in = dtmp("tp_in", [128, 16384], BF16)
    tp_all = dtmp("tp_all", [NCORE * 128, 16384], BF16)
    xsp = dtmp("xsp", [128, 8 * NT], F32)
    t_k_in, t_k_all, t_v_in, t_v_all = Tile(None), Tile(None), Tile(None), Tile(None)
    t_h_in, t_h_all, t_tp_in, t_tp_all, t_xsp = Tile(None), Tile(None), Tile(None), Tile(None), Tile(None)
    t_out = Tile(None)
    ag_in = dtmp("ag_in", [128, 24], F32)
    ag_out = dtmp("ag_out", [NCORE * 128, 24], F32)
    t_ag_in, t_ag_out = Tile(None), Tile(None)

    dbg = {}

    def dbg_out(name, shape, dt=F32):
        dbg[name] = nc.dram_tensor("dbg_" + name, list(shape), dt, kind="ExternalOutput")
        return dbg[name]

    TOTAL = 53200
    PERS = 6000
    XR = 8 * NT
    big = stack.enter_context(nc.sbuf_tensor("big", [128, TOTAL], F32))
    psum = stack.enter_context(nc.psum_tensor("psum", [128, 4096], F32))
    pmem = Mem(big, 0, PERS, kb)
    memx = Mem(big, PERS, PERS + XR, kb)
    mem = Mem(big, PERS + XR, TOTAL, kb)
    PS = [Tile(psum[:, 512 * b:512 * (b + 1)]) for b in range(8)]
    psrr = [0]

    def nextps():
        b = psrr[0] % 8
        psrr[0] += 1
        return PS[b]

    def taps():
        return [(s, c) for q in kb.dmapool for s, c in kb.dmapool[q] if c > 0]

    consts = Tile(pmem.f32(5 * 128))
    ident = consts.ap[:, 0:128]
    ones = consts.ap[:, 128:256]
    blk = consts.ap[:, 256:384]
    pm = consts.ap[:, 384:512]
    sel = consts.ap[:, 512:576]
    vecs = Tile(pmem.f32(NVEC))
    rope = Tile(pmem.f32(2 * NT))
    masks = Tile(pmem.f32(2))
    mod = Tile(pmem.f32(2 * 48 * 2))
    der = Tile(pmem.f32(64))
    epsb = Tile(pmem.f32(8))
    cbf = Tile(pmem.bf16(256))
    kb.dma("sync", consts.ap, consts_d.ap(), writes=[consts])
    kb.dma("sync", vecs.ap, vecs_d.ap(), writes=[vecs])
    kb.dma("sync", rope.ap, rope_d.ap(), writes=[rope])
    kb.dma("sync", masks.ap, masks_d.ap(), writes=[masks])
    kb.op("gpsimd", lambda e: e.tensor_copy(out=cbf.ap, in_=consts.ap[:, 0:256]), [consts], [cbf])
    ident_bf = cbf.ap[:, 0:128]
    ones_bf = cbf.ap[:, 128:256]
    kb.op("vector", lambda e: e.memset(epsb.ap[:, 0:1], EPS), [], [epsb])
    kb.op("vector", lambda e: e.memset(epsb.ap[:, 1:2], LN_EPS), [], [epsb])
    kb.op("vector", lambda e: e.tensor_scalar(out=epsb.ap[:, 2:3], in0=vecs.ap[:, V_QK:V_QK + 1], scalar1=0.125,
                                              scalar2=None, op0=ALU.mult), [vecs], [epsb])
    kb.op("vector", lambda e: e.tensor_copy(out=epsb.ap[:, 3:4], in_=vecs.ap[:, V_QK + 1:V_QK + 2]), [vecs], [epsb])

    def modv(l, w, c, j=0):
        o = ((l * 48) + w * 8 + c) * 2 + j
        return mod.ap[:, o:o + 1]

    def vcol(base, c):
        return vecs.ap[:, base + c:base + c + 1]

    xT_ap = memx.f32(8 * NT)
    XT = [[Tile(xT_ap[:, c * NT + a:c * NT + b]) for (a, b) in TT] for c in range(8)]
    ALLXT = [t for row in XT for t in row]

    def xt_cols(c, a, b):
        return xT_ap[:, c * NT + a:c * NT + b]

    m0 = mem.mark()
    cT = Tile(mem.f32(16))
    kb.dma("sync", cT.ap, cT_d.ap(), writes=[cT])
    sc = Tile(mem.f32(16))
    kb.op("scalar", lambda e: e.activation(out=sc.ap, in_=cT.ap, func=AF.Sigmoid), [cT], [sc])
    kb.op("vector", lambda e: e.tensor_tensor(out=sc.ap, in0=sc.ap, in1=cT.ap, op=ALU.mult), [sc, cT], [sc])
    wst = [Tile(mem.f32(8 * 768)) for _ in range(2)]
    moc = Tile(mem.f32(24))
    modraw = Tile(mem.f32(8 * 24))
    for l in range(2):
        w = wst[l]
        kb.dma("sync", w.ap.rearrange("p (kc n) -> p kc n", kc=8),
               wada_d.ap()[l].rearrange("(kc p) n -> p kc n", p=128), writes=[w])
        pst = nextps()
        for cc in range(6):
            for kc in range(8):
                kb.op("tensor", (lambda e, w=w, pst=pst, cc=cc, kc=kc: e.matmul(
                    pst.ap[:, cc * 2:cc * 2 + 2], lhsT=w.ap[:, kc * 768 + cc * 128:kc * 768 + cc * 128 + 128],
                    rhs=sc.ap[:, kc * 2:kc * 2 + 2], start=(kc == 0), stop=(kc == 7))),
                    [w, sc], [pst], sig=(cc == 5 and kc == 7))
        kb.op("vector", (lambda e, pst=pst, l=l: e.tensor_copy(out=moc.ap[:, l * 12:l * 12 + 12], in_=pst.ap[:, 0:12])),
              [pst], [moc])
    kb.dma("sync", ag_in.ap(), moc.ap, reads=[moc], writes=[t_ag_in])
    kb.allgather(t_ag_in, t_ag_out, ag_in, ag_out)
    kb.dma("sync", modraw.ap.rearrange("p (r f) -> p r f", r=8), ag_out.ap().rearrange("(r p) f -> p r f", p=128),
           reads=[t_ag_out], writes=[modraw])
    for l in range(2):
        for j in range(2):
            kb.op("vector", (lambda e, l=l, j=j: e.tensor_tensor(
                out=mod.ap[:, l * 96 + j:l * 96 + j + 95:2].rearrange("p (r c) -> p r c", r=8),
                in0=modraw.ap.rearrange("p (r f) -> p r f", r=8)[:, :, l * 12 + j:l * 12 + j + 11:2],
                in1=vecs.ap[:, V_BADA + l * 48:V_BADA + l * 48 + 48].rearrange("p (r c) -> p r c", r=8),
                op=ALU.add)), [modraw, vecs], [mod])
    mem.release(m0)

    def derv(l, kind, c):
        o = (l * 3 + kind) * 8 + c
        return der.ap[:, o:o + 1]

    for l in range(2):
        for kind, (w, j, gb) in enumerate([(1, 0, V_GMIX), (1, 1, V_GMIX), (4, 0, V_GFFN)]):
            o = (l * 3 + kind) * 8
            mo = (l * 48 + w * 8) * 2 + j
            kb.op("vector", (lambda e, o=o, mo=mo, gb=gb, l=l: e.scalar_tensor_tensor(
                out=der.ap[:, o:o + 8], in0=mod.ap[:, mo:mo + 15:2], scalar=1.0,
                in1=vecs.ap[:, gb + l * 8:gb + l * 8 + 8], op0=ALU.add, op1=ALU.mult)),
                [mod, vecs], [der])
    kb.op("vector", lambda e: e.tensor_tensor(
        out=der.ap[:, 48:56], in0=vecs.ap[:, V_BPW2:V_BPW2 + 8],
        in1=mod.ap[:, (48 + 16) * 2:(48 + 16) * 2 + 15:2], op=ALU.mult), [mod, vecs], [der])

    def norm_to_hT(src_cols, ntiles, G, S, hT_tiles, hT_cols):
        mm = mem.mark()
        sq = [Tile(mem.f32(418)) for _ in range(2)]
        rs = [Tile(mem.f32(418)) for _ in range(2)]
        tm = [Tile(mem.f32(418)) for _ in range(2)]
        cnt = 0
        for ti, (a, b, srct) in enumerate(ntiles):
            n = b - a
            pst = nextps()
            for c in range(8):
                s = sq[cnt % 2]
                cnt += 1
                kb.op("scalar", (lambda e, s=s, c=c, a=a, b=b, n=n: e.activation(
                    out=s.ap[:, 0:n], in_=src_cols(c, a, b), func=AF.Square)), [srct[c]], [s])
                kb.op("tensor", (lambda e, s=s, c=c, pst=pst, n=n: e.matmul(
                    pst.ap[:, 0:n], lhsT=ones, rhs=s.ap[:, 0:n], start=(c == 0), stop=(c == 7))),
                    [s, consts], [pst], sig=True)
            r = rs[ti % 2]
            kb.op("scalar", (lambda e, r=r, pst=pst, n=n: e.activation(
                out=r.ap[:, 0:n], in_=pst.ap[:, 0:n], func=AF.Sqrt, scale=1.0 / D, bias=epsb.ap[:, 0:1])),
                [pst, epsb], [r])
            kb.op("vector", (lambda e, r=r, n=n: e.reciprocal(out=r.ap[:, 0:n], in_=r.ap[:, 0:n])), [r], [r])
            for c in range(8):
                t = tm[c % 2]
                kb.op("vector", (lambda e, t=t, c=c, a=a, b=b, n=n, r=r: e.tensor_tensor(
                    out=t.ap[:, 0:n], in0=src_cols(c, a, b), in1=r.ap[:, 0:n], op=ALU.mult)),
                    [srct[c], r], [t])
                kb.op("scalar", (lambda e, t=t, c=c, a=a, b=b, n=n: e.activation(
                    out=hT_cols(c, a, b), in_=t.ap[:, 0:n], func=AF.Identity, scale=G(c), bias=S(c))),
                    [t, mod, der], [hT_tiles[c][ti]])
        mem.release(mm)

    def load_transposed(src_d, nrows_total, dst_ap, dst_stride, dst_tiles_fn):
        mm = mem.mark()
        xin = [Tile(mem.f32(D)) for _ in range(2)]
        nt = (nrows_total + 127) // 128
        for t in range(nt):
            xi = xin[t % 2]
            rows = min(128, nrows_total - t * 128)
            kb.dma("gpsimd", xi.ap[0:rows, :], src_d.ap()[t * 128:t * 128 + rows, :], writes=[xi])
            a0 = t * 128
            for hlf in range(2):
                pst = nextps()
                for cc in range(4):
                    c = hlf * 4 + cc
                    kb.op("tensor", (lambda e, xi=xi, pst=pst, cc=cc, c=c, rows=rows: e.transpose(
                        out=pst.ap[:, cc * 128:cc * 128 + rows], in_=xi.ap[0:rows, c * 128:(c + 1) * 128],
                        identity=ident[0:rows, 0:rows])), [xi, consts], [pst], sig=(cc == 3))
                wt = dst_tiles_fn(hlf, a0, a0 + rows)
                kb.op("scalar", (lambda e, pst=pst, hlf=hlf, a0=a0, rows=rows: e.activation(
                    out=bass.AP(dst_ap.tensor, dst_ap.offset + hlf * 4 * dst_stride + a0,
                                [list(dst_ap.ap[0]), [dst_stride, 4], [1, rows]]),
                    in_=bass.AP(pst.ap.tensor, pst.ap.offset, [list(pst.ap.ap[0]), [128, 4], [1, rows]]),
                    func=AF.Identity)), [pst], wt)
        mem.release(mm)

    def xt_tiles_touch(hlf, a0, b0):
        return [XT[hlf * 4 + cc][ti] for cc in range(4) for ti, (a, b) in enumerate(TT) if a < b0 and b > a0]

    def tt_list(tiles):
        return [(a, b, [tiles[c][ti] for c in range(8)]) for ti, (a, b) in enumerate(TT)]

    def cast_load(dst_tile, dst_ap3, src_ap3, stage_tiles, idx, stage_view):
        st = stage_tiles[idx % len(stage_tiles)]
        kb.dma("sync", stage_view(st), src_ap3, writes=[st])
        kb.op("gpsimd", lambda e: e.tensor_copy(out=dst_ap3, in_=stage_view(st)), [st], [dst_tile])

    load_transposed(x_d, NT, xT_ap, NT, xt_tiles_touch)
    if debug == "s1":
        o = dbg_out("xT", [128, 8 * NT])
        kb.dma("sync", o.ap(), xT_ap, reads=ALLXT, writes=[t_out])
        o2 = dbg_out("mod", [128, 192])
        kb.dma("sync", o2.ap(), mod.ap, reads=[mod], writes=[t_out])
        kb.emit_all(taps())
        return
    hT_ap = mem.bf16(8 * NT)
    HT = [[Tile(hT_ap[:, c * NT + a:c * NT + b]) for (a, b) in TT] for c in range(8)]
    ALLHT = [t for row in HT for t in row]

    def ht_cols(c, a, b):
        return hT_ap[:, c * NT + a:c * NT + b]
    norm_to_hT(xt_cols, tt_list(XT), lambda c: derv(0, 0, c), lambda c: modv(0, 0, c, 0), HT, ht_cols)
    for c in range(8):
        kb.dma("sync", h_in.ap()[c * 128:(c + 1) * 128, :], hT_ap[:, c * NT + HALO:c * NT + HALO + OWN],
               reads=HT[c], writes=[t_h_in])
    kb.barrier()
    memx.top = memx.base

    QZ = Tile(memx.bf16(8 * NT))
    kb.op("gpsimd", lambda e: e.memset(QZ.ap, 0.0), [], [QZ])
    wfb = Tile(memx.bf16(8 * 64))
    kcb = Tile(memx.bf16(CTX))
    vctx = Tile(memx.bf16(2 * 130))
    mxkeep = memx.mark()
    cx_ap = memx.f32(8 * CTX)
    CXT = [[Tile(cx_ap[:, c * CTX:(c + 1) * CTX])] for c in range(8)]
    load_transposed(ctx_d, CTX, cx_ap, CTX, lambda hlf, a0, b0: [CXT[hlf * 4 + cc][0] for cc in range(4)])
    hc_ap = memx.bf16(8 * CTX)
    HCT = [[Tile(hc_ap[:, c * CTX:(c + 1) * CTX])] for c in range(8)]
    norm_to_hT(lambda c, a, b: cx_ap[:, c * CTX + a:c * CTX + b], [(0, CTX, [CXT[c][0] for c in range(8)])],
               lambda c: derv(0, 1, c), lambda c: modv(0, 0, c, 1), HCT,
               lambda c, a, b: hc_ap[:, c * CTX + a:c * CTX + b])

    win = Tile(memx.bf16(8 * 768))
    mst = mem.mark()
    stg = [Tile(mem.f32(768)) for _ in range(2)]
    for kc in range(8):
        st = stg[kc % 2]
        kb.dma("sync", st.ap, win_d.ap()[kc * 128:(kc + 1) * 128, 0:768], writes=[st])
        kb.op("gpsimd", (lambda e, st=st, kc=kc: e.tensor_copy(
            out=win.ap[:, kc * 768:kc * 768 + 512].rearrange("p (c g d) -> p c g d", c=4, g=2),
            in_=st.ap[:, 0:512].rearrange("p (g c d) -> p c g d", g=2, c=4))), [st], [win])
        kb.op("gpsimd", (lambda e, st=st, kc=kc: e.tensor_copy(
            out=win.ap[:, kc * 768 + 512:kc * 768 + 768], in_=st.ap[:, 512:768])), [st], [win])
    st = stg[0]
    kb.dma("sync", st.ap[:, 0:512].rearrange("p (kc n) -> p kc n", kc=8),
           wf_d.ap().rearrange("(kc p) n -> p kc n", p=128), writes=[st])
    kb.op("gpsimd", lambda e, st0=stg[0]: e.tensor_copy(out=wfb.ap, in_=st0.ap[:, 0:512]), [stg[0]], [wfb])
    mem.release(mst)
    kb.allgather(t_h_in, t_h_all, h_in, h_all)

    KTl = Tile(mem.bf16(NT))
    vloc = Tile(mem.bf16(16 * 130))
    kb.op("gpsimd", lambda e: e.memset(vloc.ap, 1.0), [], [vloc])
    kb.op("gpsimd", lambda e: e.memset(vctx.ap, 1.0), [], [vctx])
    NB = 3
    qraw = [Tile(mem.f32(418)) for _ in range(NB)]
    sqt = [Tile(mem.f32(418)) for _ in range(NB)]
    rst = [Tile(mem.f32(418)) for _ in range(NB)]
    qgt = [Tile(mem.f32(418)) for _ in range(NB)]
    t1t = [Tile(mem.f32(418)) for _ in range(NB)]
    t2t = [Tile(mem.f32(418)) for _ in range(NB)]
    it = [0]

    def qk_block(wcol, rhs_fn, rtiles, n, gaincol, ropecols, out_fns):
        i = it[0] % NB
        it[0] += 1
        ps = nextps()
        for kc in range(8):
            kb.op("tensor", (lambda e, kc=kc, ps=ps: e.matmul(
                ps.ap[:, 0:n], lhsT=win.ap[:, kc * 768 + wcol:kc * 768 + wcol + 128], rhs=rhs_fn(kc),
                start=(kc == 0), stop=(kc == 7))), [win] + rtiles, [ps], sig=(kc == 7))
        qr, sq, r, qg, t1, t2 = qraw[i], sqt[i], rst[i], qgt[i], t1t[i], t2t[i]
        kb.op("scalar", lambda e: e.activation(out=qr.ap[:, 0:n], in_=ps.ap[:, 0:n], func=AF.Identity), [ps], [qr])
        kb.op("scalar", lambda e: e.activation(out=sq.ap[:, 0:n], in_=ps.ap[:, 0:n], func=AF.Square), [ps], [sq])
        ps2 = nextps()
        kb.op("tensor", lambda e: e.matmul(ps2.ap[:, 0:n], lhsT=blk, rhs=sq.ap[:, 0:n], start=True, stop=True),
              [sq, consts], [ps2])
        kb.op("scalar", lambda e: e.activation(out=r.ap[:, 0:n], in_=ps2.ap[:, 0:n], func=AF.Sqrt, scale=1.0 / 64,
                                               bias=epsb.ap[:, 0:1]), [ps2, epsb], [r])
        kb.op("vector", lambda e: e.reciprocal(out=r.ap[:, 0:n], in_=r.ap[:, 0:n]), [r], [r])
        kb.op("vector", lambda e: e.scalar_tensor_tensor(out=qg.ap[:, 0:n], in0=qr.ap[:, 0:n], scalar=gaincol,
                                                         in1=r.ap[:, 0:n], op0=ALU.mult, op1=ALU.mult),
              [qr, r, epsb], [qg])
        if ropecols is None:
            for (oap, otile, p0, p1) in out_fns:
                kb.op("vector", (lambda e, oap=oap, p0=p0, p1=p1: e.tensor_copy(out=oap, in_=qg.ap[p0:p1, 0:n])),
                      [qg], [otile])
            return
        a, b = ropecols
        ps3 = nextps()
        kb.op("tensor", lambda e: e.matmul(ps3.ap[:, 0:n], lhsT=pm, rhs=qg.ap[:, 0:n], start=True, stop=True),
              [qg, consts], [ps3])
        kb.op("gpsimd", lambda e: e.tensor_tensor(out=t1.ap[:, 0:n], in0=qg.ap[:, 0:n], in1=rope.ap[:, a:b],
                                                  op=ALU.mult), [qg, rope], [t1])
        kb.op("vector", lambda e: e.tensor_tensor(out=t2.ap[:, 0:n], in0=ps3.ap[:, 0:n],
                                                  in1=rope.ap[:, NT + a:NT + b], op=ALU.mult), [ps3, rope], [t2])
        for (oap, otile, p0, p1) in out_fns:
            kb.op("vector", (lambda e, oap=oap, p0=p0, p1=p1: e.tensor_tensor(
                out=oap, in0=t1.ap[p0:p1, 0:n], in1=t2.ap[p0:p1, 0:n], op=ALU.add)), [t1, t2], [otile])

    for qc in range(5):
        for ti, (a, b) in enumerate(TT):
            n = b - a
            rhs_fn = (lambda kc, a=a, b=b: hT_ap[:, kc * NT + a:kc * NT + b])
            rt = [HT[kc][ti] for kc in range(8)]
            if qc < 4:
                outs = [(QZ.ap[0:64, qc * NT + a:qc * NT + b], QZ, 0, 64),
                        (QZ.ap[64:128, (4 + qc) * NT + a:(4 + qc) * NT + b], QZ, 64, 128)]
                qk_block(qc * 128, rhs_fn, rt, n, epsb.ap[:, 2:3], (a, b), outs)
            else:
                outs = [(KTl.ap[:, a:b], KTl, 0, 128)]
                qk_block(512, rhs_fn, rt, n, epsb.ap[:, 3:4], (a, b), outs)
    qk_block(512, lambda kc: hc_ap[:, kc * CTX:(kc + 1) * CTX], [HCT[kc][0] for kc in range(8)], CTX,
             epsb.ap[:, 3:4], None, [(kcb.ap[:, 0:CTX], kcb, 0, 128)])
    for t in range(18):
        ps = nextps()
        if t < 16:
            lt = (lambda kc, t=t: hT_ap[:, kc * NT + HALO + 128 * t:kc * NT + HALO + 128 * t + 128])
            rt = ALLHT
            dst, dtile = vloc.ap[:, t * 130:(t + 1) * 130], vloc
        else:
            lt = (lambda kc, t=t: hc_ap[:, kc * CTX + (t - 16) * 128:kc * CTX + (t - 16) * 128 + 128])
            rt = [HCT[kc][0] for kc in range(8)]
            dst, dtile = vctx.ap[:, (t - 16) * 130:(t - 15) * 130], vctx
        for kc in range(8):
            kb.op("tensor", (lambda e, kc=kc, ps=ps, lt=lt: e.matmul(
                ps.ap[:, 0:128], lhsT=lt(kc), rhs=win.ap[:, kc * 768 + 640:kc * 768 + 768],
                start=(kc == 0), stop=(kc == 7))), [win] + rt, [ps], sig=(kc == 7))
        kb.op("scalar", (lambda e, ps=ps, dst=dst: e.activation(
            out=dst.rearrange("p (g d) -> p g d", g=2)[:, :, 0:64],
            in_=ps.ap[:, 0:128].rearrange("p (g d) -> p g d", g=2), func=AF.Identity)), [ps], [dtile])
    kb.dma("sync", k_in.ap(), KTl.ap[:, HALO:HALO + OWN], reads=[KTl], writes=[t_k_in])
    kb.dma("sync", v_in.ap(), vloc.ap, reads=[vloc], writes=[t_v_in])
    kb.allgather(t_k_in, t_k_all, k_in, k_all)
    kb.allgather(t_v_in, t_v_all, v_in, v_all)
    if debug == "s3":
        o = dbg_out("QZ", [128, 8 * NT], BF16)
        kb.dma("sync", o.ap(), QZ.ap, reads=[QZ], writes=[t_out])
        o = dbg_out("KTl", [128, NT], BF16)
        kb.dma("sync", o.ap(), KTl.ap, reads=[KTl], writes=[t_out])
        o = dbg_out("vloc", [128, 16 * 130], BF16)
        kb.dma("sync", o.ap(), vloc.ap, reads=[vloc], writes=[t_out])
        o = dbg_out("kall", [NCORE * 128, OWN], BF16)
        kb.dma("sync", o.ap(), k_all.ap(), reads=[t_k_all], writes=[t_out])
        o = dbg_out("hT", [128, 8 * NT], BF16)
        kb.dma("sync", o.ap(), hT_ap, reads=ALLHT, writes=[t_out])
        kb.emit_all(taps())
        return
    memx.release(mxkeep)
    mem.top = mem.base

    mf = mem.mark()
    tabs = Tile(mem.bf16(128 + 512 + 36))
    e64 = tabs.ap[0:64, 0:128]
    cs12 = tabs.ap[:, 128:640]
    cs3 = tabs.ap[:, 640:676]
    kb.dma("sync", e64, e64_d.ap(), writes=[tabs])
    kb.dma("sync", cs12, cs12_d.ap(), writes=[tabs])
    kb.dma("sync", cs3, cs3_d.ap(), writes=[tabs])
    tw = Tile(mem.f32(1024))
    kb.dma("sync", tw.ap, tw_d.ap(), writes=[tw])
    fT = Tile(mem.bf16(16384))
    mh = mem.mark()
    hblk = [Tile(mem.bf16(8 * 1024)) for _ in range(2)]
    for bi in range(16):
        r, half = bi // 2, bi % 2
        hb = hblk[bi % 2]
        kb.dma("sync", hb.ap.rearrange("p (kc t) -> p kc t", kc=8),
               h_all.ap()[r * 1024:(r + 1) * 1024, half * 1024:(half + 1) * 1024].rearrange("(kc p) t -> p kc t", p=128),
               reads=[t_h_all], writes=[hb])
        for sub in range(2):
            ps = nextps()
            for kc in range(8):
                kb.op("tensor", (lambda e, kc=kc, ps=ps, hb=hb, sub=sub: e.matmul(
                    ps.ap[0:64, 0:512], lhsT=wfb.ap[:, kc * 64:(kc + 1) * 64],
                    rhs=hb.ap[:, kc * 1024 + sub * 512:kc * 1024 + sub * 512 + 512],
                    start=(kc == 0), stop=(kc == 7))), [wfb, hb], [ps], sig=(kc == 7))
            tok0 = r * 2048 + half * 1024 + sub * 512
            eng = "scalar" if sub == 0 else "vector"
            if eng == "scalar":
                kb.op("scalar", (lambda e, ps=ps, tok0=tok0: e.activation(
                    out=fT.ap[0:64, tok0:tok0 + 512], in_=ps.ap[0:64, 0:512], func=AF.Identity)), [ps], [fT])
            else:
                kb.op("vector", (lambda e, ps=ps, tok0=tok0: e.tensor_copy(
                    out=fT.ap[0:64, tok0:tok0 + 512], in_=ps.ap[0:64, 0:512])), [ps], [fT])
    mem.release(mh)
    Z = Tile(mem.bf16(16384))
    for q4 in range(32):
        ps = nextps()
        for qq in range(4):
            l1 = q4 * 4 + qq
            kb.op("tensor", (lambda e, ps=ps, qq=qq, l1=l1: e.matmul(
                ps.ap[:, qq * 128:(qq + 1) * 128], lhsT=fT.ap[0:64, l1:16384:128], rhs=e64,
                start=True, stop=True)), [fT, tabs], [ps], sig=(qq == 3))
        eng = "scalar" if q4 % 2 == 0 else "vector"
        zout = Z.ap.rearrange("p (x l) -> p l x", l=128)[:, q4 * 4:q4 * 4 + 4, :]
        zin = ps.ap[:, 0:512].rearrange("p (q x) -> p q x", q=4)
        if eng == "scalar":
            kb.op("scalar", (lambda e, zout=zout, zin=zin: e.activation(out=zout, in_=zin, func=AF.Identity)),
                  [ps], [Z])
        else:
            kb.op("vector", (lambda e, zout=zout, zin=zin: e.tensor_copy(out=zout, in_=zin)), [ps], [Z])
    p1t = [Tile(mem.f32(512)) for _ in range(2)]
    p2t = [Tile(mem.f32(512)) for _ in range(2)]
    tps = [Tile(mem.bf16(512)) for _ in range(2)]
    tp4 = tp_in.ap().rearrange("p (r m k) -> p r m k", r=2, m=64)
    for mb in range(32):
        ps = nextps()
        for mm_ in range(2):
            m = mb * 2 + mm_
            kb.op("tensor", (lambda e, ps=ps, mm_=mm_, m=m: e.matmul(
                ps.ap[:, mm_ * 256:(mm_ + 1) * 256], lhsT=Z.ap[:, m * 128:(m + 1) * 128], rhs=cs12[:, 0:256],
                start=True, stop=False)), [Z, tabs], [ps], sig=False)
            kb.op("tensor", (lambda e, ps=ps, mm_=mm_, m=m: e.matmul(
                ps.ap[:, mm_ * 256:(mm_ + 1) * 256], lhsT=Z.ap[:, (64 + m) * 128:(65 + m) * 128], rhs=cs12[:, 256:512],
                start=False, stop=True)), [Z, tabs], [ps], sig=(mm_ == 1))
        p1, p2, tp = p1t[mb % 2], p2t[mb % 2], tps[mb % 2]
        kb.op("vector", (lambda e, ps=ps, p1=p1: e.tensor_tensor(out=p1.ap, in0=ps.ap[:, 0:512], in1=tw.ap[:, 0:512],
                                                               op=ALU.mult)), [ps, tw], [p1])
        kb.op("vector", (lambda e, ps=ps, p2=p2: e.tensor_tensor(out=p2.ap, in0=ps.ap[:, 0:512], in1=tw.ap[:, 512:1024],
                                                               op=ALU.mult)), [ps, tw], [p2])
        p1v = p1.ap.rearrange("p (m r k) -> p m r k", m=2, r=2)
        p2v = p2.ap.rearrange("p (m r k) -> p m r k", m=2, r=2)
        tpv = tp.ap.rearrange("p (r m k) -> p r m k", r=2, m=2)
        kb.op("gpsimd", (lambda e, p1v=p1v, p2v=p2v, tpv=tpv: e.tensor_tensor(
            out=tpv[:, 0, :, :], in0=p1v[:, :, 0, :], in1=p2v[:, :, 1, :], op=ALU.add)), [p1, p2], [tp])
        kb.op("gpsimd", (lambda e, p1v=p1v, p2v=p2v, tpv=tpv: e.tensor_tensor(
            out=tpv[:, 1, :, :], in0=p1v[:, :, 1, :], in1=p2v[:, :, 0, :], op=ALU.subtract)), [p1, p2], [tp])
        kb.dma("sync", tp4[:, :, mb * 2:mb * 2 + 2, :], tpv, reads=[tp], writes=[t_tp_in])
    kb.allgather(t_tp_in, t_tp_all, tp_in, tp_all)
    mem.release(mf)
    aT = Tile(mem.bf16_top(8 * NT))

    KT = [Tile(mem.bf16(2048)) for _ in range(8)] + [Tile(mem.bf16(CTX))]
    kt_base = KT[0].ap
    KTall = bass.AP(kt_base.tensor, kt_base.offset, [list(kt_base.ap[0]), [1, 16384 + CTX]])
    VA = [Tile(mem.bf16(16 * 130)) for _ in range(8)] + [Tile(mem.bf16(2 * 130))]
    va_base = VA[0].ap
    VAall = bass.AP(va_base.tensor, va_base.offset, [list(va_base.ap[0]), [1, 130 * 130]])
    for r in range(8):
        kb.dma("gpsimd", KT[r].ap, k_all.ap()[r * 128:(r + 1) * 128, :], reads=[t_k_all], writes=[KT[r]])
        kb.dma("gpsimd", VA[r].ap, v_all.ap()[r * 128:(r + 1) * 128, :], reads=[t_v_all], writes=[VA[r]])
    kb.op("gpsimd", lambda e: e.tensor_copy(out=KT[8].ap, in_=kcb.ap), [kcb], [KT[8]])
    kb.op("gpsimd", lambda e: e.tensor_copy(out=VA[8].ap, in_=vctx.ap), [vctx], [VA[8]])
    PT = [Tile(mem.bf16(2 * 418)) for _ in range(4)]
    PSS = [Tile(psum[:, 1024 + 1024 * k:2048 + 1024 * k]) for k in range(3)]
    rr = Tile(mem.f32(418))
    kb.op("gpsimd", lambda e: e.memset(rr.ap, 0.0), [], [rr])
    osb = [Tile(mem.f32(418)) for _ in range(2)]
    for c in range(4):
        heads = (c, 4 + c)
        for ti, (a, b) in enumerate(TT):
            n = b - a
            for kc in range(NKC + 2):
                if kc < NKC:
                    ktile = KT[kc // 16] if kc < 128 else KT[8]
                    pss = PSS[kc % 3]
                    pt = PT[kc % 4]
                    for hi in range(2):
                        h = heads[hi]
                        kb.op("tensor", (lambda e, pss=pss, kc=kc, h=h, hi=hi, n=n, a=a, b=b: e.matmul(
                            pss.ap[:, hi * 512:hi * 512 + n], lhsT=KTall[:, kc * 128:(kc + 1) * 128],
                            rhs=QZ.ap[:, h * NT + a:h * NT + b], start=True, stop=True)), [ktile, QZ], [pss],
                            sig=(hi == 1))
                    kb.op("scalar", (lambda e, pss=pss, pt=pt, n=n: e.activation(
                        out=pt.ap.rearrange("p (h m) -> p h m", h=2)[:, :, 0:n],
                        in_=pss.ap.rearrange("p (h m) -> p h m", h=2)[:, :, 0:n], func=AF.Exp)), [pss], [pt])
                if kc >= 2:
                    k2 = kc - 2
                    vtile = VA[k2 // 16] if k2 < 128 else VA[8]
                    pt = PT[k2 % 4]
                    for hi in range(2):
                        kb.op("tensor", (lambda e, hi=hi, k2=k2, pt=pt, n=n: e.matmul(
                            PS[hi].ap[0:65, 0:n], lhsT=VAall[:, k2 * 130 + hi * 65:k2 * 130 + hi * 65 + 65],
                            rhs=pt.ap[:, hi * 418:hi * 418 + n], start=(k2 == 0), stop=(k2 == NKC - 1))),
                            [vtile, pt], [PS[hi]], sig=(k2 == NKC - 1 and hi == 1))
            for hi in range(2):
                h = heads[hi]
                ob = osb[hi]
                kb.op("vector", (lambda e, hi=hi, n=n: e.reciprocal(out=rr.ap[64:65, 0:n], in_=PS[hi].ap[64:65, 0:n])),
                      [PS[hi]], [rr])
                kb.op("tensor", (lambda e, n=n: e.matmul(PSS[0].ap[0:64, 0:n], lhsT=sel, rhs=rr.ap[:, 0:n],
                                                         start=True, stop=True)), [rr, consts], [PSS[0]])
                kb.op("scalar", (lambda e, hi=hi, ob=ob, n=n: e.activation(out=ob.ap[0:64, 0:n], in_=PS[hi].ap[0:64, 0:n],
                                                                      func=AF.Identity)), [PS[hi]], [ob])
                kb.op("vector", (lambda e, h=h, ob=ob, n=n, a=a, b=b: e.tensor_tensor(
                    out=aT.ap[0:64, h * NT + a:h * NT + b], in0=ob.ap[0:64, 0:n], in1=PSS[0].ap[0:64, 0:n],
                    op=ALU.mult)), [ob, PSS[0]], [aT])
    memx.top = memx.base
    mem.top = mem.base
    kb.barrier()
    YT = Tile(mem.bf16_top(8 * 2304))
    if debug == "s5":
        o = dbg_out("aT", [64, 8 * NT], BF16)
        kb.dma("sync", o.ap(), aT.ap[0:64, :], reads=[aT], writes=[t_out])
        kb.emit_all(taps())
        return

    tabs2 = Tile(mem.bf16(36))
    kb.dma("sync", tabs2.ap, cs3_d.ap(), writes=[tabs2])
    tpg = [Tile(memx.bf16(16384)) for _ in range(2)]
    for g in range(8):
        tb = tpg[g % 2]
        kb.dma("gpsimd", tb.ap, tp_all.ap()[g * 128:(g + 1) * 128, :], reads=[t_tp_all], writes=[tb])
        k2_0 = 0
        while k2_0 < 128:
            nk = min(28, 128 - k2_0)
            ps = nextps()
            for kk in range(nk):
                k2 = k2_0 + kk
                kb.op("tensor", (lambda e, ps=ps, kk=kk, k2=k2, tb=tb: e.matmul(
                    ps.ap[0:64, kk * 18:kk * 18 + 18], lhsT=tb.ap[:, k2:8192:128], rhs=tabs2.ap[:, 0:18],
                    start=True, stop=False)), [tb, tabs2], [ps], sig=False)
                kb.op("tensor", (lambda e, ps=ps, kk=kk, k2=k2, tb=tb: e.matmul(
                    ps.ap[0:64, kk * 18:kk * 18 + 18], lhsT=tb.ap[:, 8192 + k2:16384:128], rhs=tabs2.ap[:, 18:36],
                    start=False, stop=True)), [tb, tabs2], [ps], sig=(kk == nk - 1))
            yout = YT.ap[0:64, g * 2304:(g + 1) * 2304].rearrange("p (k1 k2) -> p k2 k1", k2=128)[:, k2_0:k2_0 + nk, :]
            yin = ps.ap[0:64, 0:nk * 18].rearrange("p (k a) -> p k a", a=18)
            kb.op("scalar", (lambda e, yout=yout, yin=yin: e.activation(out=yout, in_=yin, func=AF.Identity,
                                                                        scale=1.0 / 1024.0)), [ps], [YT])
            k2_0 += nk
    memx.top = memx.base
    kb.barrier()
    if debug == "s6":
        o = dbg_out("YT", [64, 8 * 2304], BF16)
        kb.dma("sync", o.ap(), YT.ap[0:64, :], reads=[YT], writes=[t_out])
        o = dbg_out("aT", [64, 8 * NT], BF16)
        kb.dma("sync", o.ap(), aT.ap[0:64, :], reads=[aT], writes=[t_out])
        kb.emit_all(taps())
        return

    xT_ap2 = memx.f32(8 * NT)
    load_transposed(x_d, NT, xT_ap, NT, xt_tiles_touch)
    mo_ = mem.mark()
    wo = Tile(mem.bf16(16 * 1024))
    for ch in range(16):
        kb.dma("gpsimd", wo.ap[0:64, ch * 1024:(ch + 1) * 1024], wout_d.ap()[ch * 64:(ch + 1) * 64, :], writes=[wo])
    for nchk in range(8):
        for ti, (a, b) in enumerate(TT):
            n = b - a
            ps = nextps()
            for ch in range(16):
                if ch < 8:
                    rhs = aT.ap[0:64, ch * NT + a:ch * NT + b]
                    rt = aT
                else:
                    g = ch - 8
                    rhs = YT.ap[0:64, g * 2304 + 111 + a:g * 2304 + 111 + b]
                    rt = YT
                kb.op("tensor", (lambda e, ps=ps, ch=ch, rhs=rhs, nchk=nchk, n=n: e.matmul(
                    ps.ap[:, 0:n], lhsT=wo.ap[0:64, ch * 1024 + nchk * 128:ch * 1024 + nchk * 128 + 128], rhs=rhs,
                    start=(ch == 0), stop=(ch == 15))), [wo, rt], [ps], sig=(ch == 15))
            kb.op("vector", (lambda e, ps=ps, nchk=nchk, a=a, b=b, n=n: e.scalar_tensor_tensor(
                out=xt_cols(nchk, a, b), in0=ps.ap[:, 0:n], scalar=modv(0, 2, nchk), in1=xt_cols(nchk, a, b),
                op0=ALU.mult, op1=ALU.add)), [ps, mod, XT[nchk][ti]], [XT[nchk][ti]])
    mem.release(mo_)
    mem.release_top()
    if debug == "s7":
        o = dbg_out("xT", [128, 8 * NT])
        kb.dma("sync", o.ap(), xT_ap, reads=ALLXT, writes=[t_out])
        kb.emit_all(taps())
        return

    def mask_halo(ap_fn, tiles_fn, nchunks):
        for c in range(nchunks):
            kb.op("vector", (lambda e, c=c: e.tensor_scalar(out=ap_fn(c, 0, HALO), in0=ap_fn(c, 0, HALO),
                                                          scalar1=masks.ap[:, 0:1], scalar2=None, op0=ALU.mult)),
                  [masks] + tiles_fn(c, 0), tiles_fn(c, 0))
            kb.op("vector", (lambda e, c=c: e.tensor_scalar(out=ap_fn(c, NT - HALO, NT), in0=ap_fn(c, NT - HALO, NT),
                                                          scalar1=masks.ap[:, 1:2], scalar2=None, op0=ALU.mult)),
                  [masks] + tiles_fn(c, 4), tiles_fn(c, 4))

    def ffn(l):
        mm0 = mem.mark()
        hT_ap = mem.bf16(8 * NT)
        HT = [[Tile(hT_ap[:, c * NT + a:c * NT + b]) for (a, b) in TT] for c in range(8)]

        def hcols(c, a, b):
            return hT_ap[:, c * NT + a:c * NT + b]
        norm_to_hT(xt_cols, tt_list(XT), lambda c: derv(l, 2, c), lambda c: modv(l, 3, c, 0), HT, hcols)
        mask_halo(hcols, lambda c, ti: [HT[c][ti]], 8)
        gT = Tile(mem.bf16(6 * NT))
        wd2 = [Tile(mem.bf16(6 * 1024)) for _ in range(2)]
        wu = [Tile(mem.bf16(8 * 256)) for _ in range(2)]
        wus = [Tile(mem.f32(8 * 256)) for _ in range(2)]
        va_ = [Tile(mem.f32(420)) for _ in range(2)]
        vb_ = [Tile(mem.f32(420)) for _ in range(2)]
        cnt = [0, 0]
        groups = [(0, 6), (6, 12), (12, 17), (17, 22)]

        def load_wu(j):
            st = wus[j % 2]
            w_ = wu[j % 2]
            st3 = st.ap.rearrange("p (kc n) -> p kc n", kc=8)
            kb.dma("sync", st3[:, :, 0:128],
                   wup_d.ap()[l, :, j * 128:(j + 1) * 128].rearrange("(kc p) n -> p kc n", p=128), writes=[st])
            kb.dma("sync", st3[:, :, 128:256],
                   wup_d.ap()[l, :, FFN + j * 128:FFN + (j + 1) * 128].rearrange("(kc p) n -> p kc n", p=128),
                   writes=[st])
            kb.op("gpsimd", (lambda e, st=st, w_=w_: e.tensor_copy(out=w_.ap, in_=st.ap)), [st], [w_])

        def load_wd(gi):
            g0, g1 = groups[gi]
            wdt = wd2[gi % 2]
            for jj in range(g0, g1):
                kb.dma("gpsimd", wdt.ap[:, (jj - g0) * 1024:(jj - g0 + 1) * 1024],
                       wdown_d.ap()[l, jj * 128:(jj + 1) * 128, :], writes=[wdt])
        load_wu(0)
        load_wd(0)
        for gi, (j0, j1) in enumerate(groups):
            wd = wd2[gi % 2]
            for j in range(j0, j1):
                w_ = wu[j % 2]
                if j + 1 < NJ:
                    load_wu(j + 1)
                if j == j0 + 1 and gi + 1 < len(groups):
                    load_wd(gi + 1)
                for ti, (a, b) in enumerate(TT):
                    n = b - a
                    a2, b2 = max(a - 1, 0), min(b + 1, NT)
                    n2 = b2 - a2
                    o = a - a2
                    rtl = sorted(set([ti] + ([ti - 1] if ti > 0 else []) + ([ti + 1] if ti < 4 else [])))
                    rt = [HT[kc][t_] for kc in range(8) for t_ in rtl]
                    vv = []
                    for half in range(2):
                        ps = nextps()
                        for kc in range(8):
                            kb.op("tensor", (lambda e, ps=ps, kc=kc, half=half, w_=w_, a2=a2, b2=b2, n2=n2: e.matmul(
                                ps.ap[:, 0:n2], lhsT=w_.ap[:, kc * 256 + half * 128:kc * 256 + half * 128 + 128],
                                rhs=hT_ap[:, kc * NT + a2:kc * NT + b2], start=(kc == 0), stop=(kc == 7))),
                                [w_] + rt, [ps], sig=(kc == 7))
                        chn = j if half == 0 else NJ + j
                        v = (va_ if half == 0 else vb_)[cnt[0] % 2]
                        vv.append(v)
                        wt = [vcol(V_WFDW + (l * 3 + k) * 44, chn) for k in range(3)]
                        bcol = vcol(V_BFDW + l * 44, chn)
                        kb.op("scalar", (lambda e, ps=ps, v=v, o=o, n=n, wt=wt, bcol=bcol: e.activation(
                            out=v.ap[:, 0:n], in_=ps.ap[:, o:o + n], func=AF.Identity, scale=wt[1], bias=bcol)),
                            [ps, vecs], [v])
                        lo = 1 if a == 0 else 0
                        kb.op("vector", (lambda e, ps=ps, v=v, o=o, n=n, wt=wt, lo=lo: e.scalar_tensor_tensor(
                            out=v.ap[:, lo:n], in0=ps.ap[:, o - 1 + lo:o - 1 + n], scalar=wt[0], in1=v.ap[:, lo:n],
                            op0=ALU.mult, op1=ALU.add)), [ps, vecs, v], [v])
                        hi_ = n - 1 if b == NT else n
                        kb.op("vector", (lambda e, ps=ps, v=v, o=o, hi_=hi_, wt=wt: e.scalar_tensor_tensor(
                            out=v.ap[:, 0:hi_], in0=ps.ap[:, o + 1:o + 1 + hi_], scalar=wt[2], in1=v.ap[:, 0:hi_],
                            op0=ALU.mult, op1=ALU.add)), [ps, vecs, v], [v])
                    cnt[0] += 1
                    v_a, v_b = vv
                    kb.op("scalar", (lambda e, v_a=v_a, n=n: e.activation(out=v_a.ap[:, 0:n], in_=v_a.ap[:, 0:n],
                                                                        func=AF.Silu)), [v_a], [v_a])
                    kb.op("gpsimd", (lambda e, v_a=v_a, v_b=v_b, n=n, j=j, j0=j0, a=a, b=b: e.tensor_tensor(
                        out=gT.ap[:, (j - j0) * NT + a:(j - j0) * NT + b], in0=v_a.ap[:, 0:n], in1=v_b.ap[:, 0:n],
                        op=ALU.mult)), [v_a, v_b], [gT])
            for nchk in range(8):
                for ti, (a, b) in enumerate(TT):
                    n = b - a
                    ps = nextps()
                    for j in range(j0, j1):
                        kb.op("tensor", (lambda e, ps=ps, j=j, j0=j0, nchk=nchk, a=a, b=b, n=n, wd=wd: e.matmul(
                            ps.ap[:, 0:n], lhsT=wd.ap[:, (j - j0) * 1024 + nchk * 128:(j - j0) * 1024 + nchk * 128 + 128],
                            rhs=gT.ap[:, (j - j0) * NT + a:(j - j0) * NT + b], start=(j == j0), stop=(j == j1 - 1))),
                            [wd, gT], [ps], sig=(j == j1 - 1))
                    kb.op("vector", (lambda e, ps=ps, nchk=nchk, a=a, b=b, n=n: e.scalar_tensor_tensor(
                        out=xt_cols(nchk, a, b), in0=ps.ap[:, 0:n], scalar=modv(l, 5, nchk), in1=xt_cols(nchk, a, b),
                        op0=ALU.mult, op1=ALU.add)), [ps, mod, XT[nchk][ti]], [XT[nchk][ti]])
        mem.release(mm0)

    ffn(0)
    if debug == "s8":
        o = dbg_out("xT", [128, 8 * NT])
        kb.dma("sync", o.ap(), xT_ap, reads=ALLXT, writes=[t_out])
        kb.emit_all(taps())
        return

    mc = mem.mark()
    h1_ap = mem.bf16(8 * NT)
    H1 = [[Tile(h1_ap[:, c * NT + a:c * NT + b]) for (a, b) in TT] for c in range(8)]

    def hcols1(c, a, b):
        return h1_ap[:, c * NT + a:c * NT + b]
    norm_to_hT(xt_cols, tt_list(XT), lambda c: derv(1, 0, c), lambda c: modv(1, 0, c, 0), H1, hcols1)
    UW = NT + 30
    uT = Tile(mem.bf16(8 * UW))
    kb.op("gpsimd", lambda e: e.memset(uT.ap, 0.0), [], [uT])
    ssum = Tile(mem.f32(NT))
    ssq = Tile(mem.f32(NT))
    msq = Tile(mem.f32(NT))
    mcA = mem.mark()
    wu = [Tile(mem.bf16(8 * 256)) for _ in range(2)]
    wus = [Tile(mem.f32(8 * 256)) for _ in range(2)]
    sg = [Tile(mem.f32(418)) for _ in range(2)]
    cnt = 0
    def load_w1(c):
        st = wus[c % 2]
        w_ = wu[c % 2]
        st3 = st.ap.rearrange("p (kc n) -> p kc n", kc=8)
        kb.dma("sync", st3[:, :, 0:128], wpw1_d.ap()[:, c * 128:(c + 1) * 128].rearrange("(kc p) n -> p kc n", p=128),
               writes=[st])
        kb.dma("sync", st3[:, :, 128:256],
               wpw1_d.ap()[:, D + c * 128:D + (c + 1) * 128].rearrange("(kc p) n -> p kc n", p=128), writes=[st])
        kb.op("gpsimd", (lambda e, st=st, w_=w_: e.tensor_copy(out=w_.ap, in_=st.ap)), [st], [w_])
    load_w1(0)
    for c in range(8):
        w_ = wu[c % 2]
        if c + 1 < 8:
            load_w1(c + 1)
        for ti, (a, b) in enumerate(TT):
            n = b - a
            pp = []
            for half in range(2):
                ps = nextps()
                pp.append(ps)
                for kc in range(8):
                    kb.op("tensor", (lambda e, ps=ps, kc=kc, half=half, w_=w_, a=a, b=b, n=n: e.matmul(
                        ps.ap[:, 0:n], lhsT=w_.ap[:, kc * 256 + half * 128:kc * 256 + half * 128 + 128],
                        rhs=h1_ap[:, kc * NT + a:kc * NT + b], start=(kc == 0), stop=(kc == 7))),
                        [w_] + [H1[kc][ti] for kc in range(8)], [ps], sig=(kc == 7))
            s_ = sg[cnt % 2]
            cnt += 1
            kb.op("scalar", (lambda e, s_=s_, ps=pp[1], c=c, n=n: e.activation(
                out=s_.ap[:, 0:n], in_=ps.ap[:, 0:n], func=AF.Sigmoid, bias=vcol(V_BPW1 + 8, c))), [pp[1], vecs], [s_])
            kb.op("vector", (lambda e, s_=s_, ps=pp[0], c=c, a=a, b=b, n=n: e.scalar_tensor_tensor(
                out=uT.ap[:, c * UW + 15 + a:c * UW + 15 + b], in0=ps.ap[:, 0:n], scalar=vcol(V_BPW1, c),
                in1=s_.ap[:, 0:n], op0=ALU.add, op1=ALU.mult)), [pp[0], s_, vecs], [uT])
    mask_halo(lambda c, a, b: uT.ap[:, c * UW + 15 + a:c * UW + 15 + b], lambda c, ti: [uT], 8)
    if debug == "s9a":
        o = dbg_out("uT", [128, 8 * UW], BF16)
        kb.dma("sync", o.ap(), uT.ap, reads=[uT], writes=[t_out])
        kb.emit_all(taps())
        return
    mem.release(mcA)
    vT_ap = h1_ap
    VT = H1
    dg = [Tile(mem.bf16(31 * 128)) for _ in range(2)]
    vtmp = [Tile(mem.f32(418)) for _ in range(2)]
    sqb = [Tile(mem.bf16(418)) for _ in range(2)]
    cnt = 0
    for c in range(8):
        d_ = dg[c % 2]
        for k in range(31):
            kb.op("vector", (lambda e, d_=d_, k=k, c=c: e.tensor_scalar(
                out=d_.ap[:, k * 128:(k + 1) * 128], in0=ident, scalar1=vcol(V_WCDW + 8 * k, c), scalar2=None,
                op0=ALU.mult)), [consts, vecs], [d_])
        for ti, (a, b) in enumerate(TT):
            n = b - a
            ps = nextps()
            for k in range(31):
                kb.op("tensor", (lambda e, ps=ps, d_=d_, k=k, c=c, a=a, n=n: e.matmul(
                    ps.ap[:, 0:n], lhsT=d_.ap[:, k * 128:(k + 1) * 128], rhs=uT.ap[:, c * UW + a + k:c * UW + a + k + n],
                    start=(k == 0), stop=(k == 30))), [d_, uT], [ps], sig=(k == 30))
            vt = vtmp[cnt % 2]
            sb = sqb[cnt % 2]
            cnt += 1
            kb.op("scalar", (lambda e, ps=ps, vt=vt, c=c, n=n: e.activation(
                out=vt.ap[:, 0:n], in_=ps.ap[:, 0:n], func=AF.Identity, bias=vcol(V_BCDW, c))), [ps, vecs], [vt])
            kb.op("vector", (lambda e, vt=vt, c=c, a=a, b=b, n=n: e.tensor_copy(
                out=vT_ap[:, c * NT + a:c * NT + b], in_=vt.ap[:, 0:n])), [vt], [VT[c][ti]])
            kb.op("scalar", (lambda e, vt=vt, sb=sb, n=n: e.activation(out=sb.ap[:, 0:n], in_=vt.ap[:, 0:n],
                                                                      func=AF.Square)), [vt], [sb])
            ps1 = nextps()
            kb.op("tensor", (lambda e, ps1=ps1, c=c, a=a, b=b, n=n: e.matmul(
                ps1.ap[:, 0:n], lhsT=ones_bf, rhs=vT_ap[:, c * NT + a:c * NT + b], start=True, stop=True)),
                [cbf, VT[c][ti]], [ps1])
            ps2 = nextps()
            kb.op("tensor", (lambda e, ps2=ps2, sb=sb, n=n: e.matmul(
                ps2.ap[:, 0:n], lhsT=ones_bf, rhs=sb.ap[:, 0:n], start=True, stop=True)), [cbf, sb], [ps2])
            if c == 0:
                kb.op("vector", (lambda e, ps1=ps1, a=a, b=b, n=n: e.tensor_copy(out=ssum.ap[:, a:b], in_=ps1.ap[:, 0:n])),
                      [ps1], [ssum])
                kb.op("vector", (lambda e, ps2=ps2, a=a, b=b, n=n: e.tensor_copy(out=ssq.ap[:, a:b], in_=ps2.ap[:, 0:n])),
                      [ps2], [ssq])
            else:
                kb.op("vector", (lambda e, ps1=ps1, a=a, b=b, n=n: e.tensor_tensor(
                    out=ssum.ap[:, a:b], in0=ps1.ap[:, 0:n], in1=ssum.ap[:, a:b], op=ALU.add)), [ps1, ssum], [ssum])
                kb.op("vector", (lambda e, ps2=ps2, a=a, b=b, n=n: e.tensor_tensor(
                    out=ssq.ap[:, a:b], in0=ps2.ap[:, 0:n], in1=ssq.ap[:, a:b], op=ALU.add)), [ps2, ssq], [ssq])
    if debug == "s9b":
        o = dbg_out("vT", [128, 8 * NT], BF16)
        kb.dma("sync", o.ap(), vT_ap, reads=[t for row in VT for t in row], writes=[t_out])
        o = dbg_out("ssum", [128, NT])
        kb.dma("sync", o.ap(), ssum.ap, reads=[ssum], writes=[t_out])
        kb.emit_all(taps())
        return
    kb.op("vector", lambda e: e.tensor_scalar(out=ssum.ap, in0=ssum.ap, scalar1=1.0 / D, scalar2=None, op0=ALU.mult),
          [ssum], [ssum])
    mem.release(mcA)
    kb.op("vector", lambda e: e.tensor_tensor(out=msq.ap, in0=ssum.ap, in1=ssum.ap, op=ALU.mult), [ssum], [msq])
    kb.op("vector", lambda e: e.scalar_tensor_tensor(out=ssq.ap, in0=ssq.ap, scalar=1.0 / D, in1=msq.ap,
                                                     op0=ALU.mult, op1=ALU.subtract), [ssq, msq], [ssq])
    kb.op("scalar", lambda e: e.activation(out=ssq.ap, in_=ssq.ap, func=AF.Sqrt, bias=epsb.ap[:, 1:2]), [ssq, epsb], [ssq])
    kb.op("vector", lambda e: e.reciprocal(out=ssq.ap, in_=ssq.ap), [ssq], [ssq])
    tmpn = [Tile(mem.f32(418)) for _ in range(2)]
    cnt = 0
    for c in range(8):
        for ti, (a, b) in enumerate(TT):
            n = b - a
            t = tmpn[cnt % 2]
            cnt += 1
            kb.op("vector", (lambda e, t=t, c=c, a=a, b=b, n=n: e.tensor_tensor(
                out=t.ap[:, 0:n], in0=vT_ap[:, c * NT + a:c * NT + b], in1=ssum.ap[:, a:b], op=ALU.subtract)),
                [VT[c][ti], ssum], [t])
            kb.op("gpsimd", (lambda e, t=t, a=a, b=b, n=n: e.tensor_tensor(
                out=t.ap[:, 0:n], in0=t.ap[:, 0:n], in1=ssq.ap[:, a:b], op=ALU.mult)), [t, ssq], [t])
            kb.op("scalar", (lambda e, t=t, c=c, a=a, b=b, n=n: e.activation(
                out=vT_ap[:, c * NT + a:c * NT + b], in_=t.ap[:, 0:n], func=AF.Silu, scale=vcol(V_LNG, c),
                bias=vcol(V_LNB, c))), [t, vecs], [VT[c][ti]])
    if debug == "s9c":
        o = dbg_out("vT", [128, 8 * NT], BF16)
        kb.dma("sync", o.ap(), vT_ap, reads=[t for row in VT for t in row], writes=[t_out])
        kb.emit_all(taps())
        return
    w2 = Tile(mem.bf16(8 * 1024))
    for kc in range(8):
        kb.dma("gpsimd", w2.ap[:, kc * 1024:(kc + 1) * 1024], wpw2_d.ap()[kc * 128:(kc + 1) * 128, :], writes=[w2])
    cnt = 0
    for nchk in range(8):
        for ti, (a, b) in enumerate(TT):
            n = b - a
            ps = nextps()
            for kc in range(8):
                kb.op("tensor", (lambda e, ps=ps, kc=kc, nchk=nchk, a=a, b=b, n=n: e.matmul(
                    ps.ap[:, 0:n], lhsT=w2.ap[:, kc * 1024 + nchk * 128:kc * 1024 + nchk * 128 + 128],
                    rhs=vT_ap[:, kc * NT + a:kc * NT + b], start=(kc == 0), stop=(kc == 7))),
                    [w2, VT[kc][ti]], [ps], sig=(kc == 7))
            t = tmpn[cnt % 2]
            cnt += 1
            kb.op("scalar", (lambda e, ps=ps, t=t, nchk=nchk, n=n: e.activation(
                out=t.ap[:, 0:n], in_=ps.ap[:, 0:n], func=AF.Identity, scale=modv(1, 2, nchk),
                bias=der.ap[:, 48 + nchk:49 + nchk])), [ps, mod, der], [t])
            kb.op("vector", (lambda e, t=t, nchk=nchk, a=a, b=b, n=n: e.tensor_tensor(
                out=xt_cols(nchk, a, b), in0=t.ap[:, 0:n], in1=xt_cols(nchk, a, b), op=ALU.add)),
                [t, XT[nchk][ti]], [XT[nchk][ti]])
    mem.release(mc)
    if debug == "s10":
        o = dbg_out("xT", [128, 8 * NT])
        kb.dma("sync", o.ap(), xT_ap, reads=ALLXT, writes=[t_out])
        kb.emit_all(taps())
        return

    ffn(1)

    ost = [Tile(mem.f32(D)) for _ in range(2)]
    for t in range(16):
        c0 = HALO + 128 * t
        o_ = ost[t % 2]
        for hlf in range(2):
            ps = nextps()
            for cc in range(4):
                c = hlf * 4 + cc
                kb.op("tensor", (lambda e, ps=ps, cc=cc, c=c, c0=c0: e.transpose(
                    out=ps.ap[:, cc * 128:(cc + 1) * 128], in_=xT_ap[:, c * NT + c0:c * NT + c0 + 128],
                    identity=ident)), [consts] + XT[c], [ps], sig=(cc == 3))
            if hlf == 0:
                kb.op("scalar", (lambda e, ps=ps, o_=o_: e.activation(out=o_.ap[:, 0:512], in_=ps.ap[:, 0:512],
                                                                      func=AF.Identity)), [ps], [o_])
            else:
                kb.op("vector", (lambda e, ps=ps, o_=o_: e.tensor_copy(out=o_.ap[:, 512:1024], in_=ps.ap[:, 0:512])),
                      [ps], [o_])
        kb.dma("gpsimd", out_d.ap()[t * 128:(t + 1) * 128, :], o_.ap, reads=[o_], writes=[Tile(None)])
    kb.emit_all(taps())


def _vec8(v):
    v = np.asarray(v, np.float32).reshape(-1, 128)
    return np.ascontiguousarray(v.T)


def _prep_shared(inp):
    sh = {}
    vecs = np.zeros((128, NVEC), np.float32)
    for l in range(2):
        vecs[:, V_GMIX + 8 * l:V_GMIX + 8 * l + 8] = _vec8(inp["g_mix"][l])
        vecs[:, V_GFFN + 8 * l:V_GFFN + 8 * l + 8] = _vec8(inp["g_ffn"][l])
        vecs[:, V_BADA + 48 * l:V_BADA + 48 * l + 48] = _vec8(inp["b_ada"][l])
        vecs[:, V_BFDW + 44 * l:V_BFDW + 44 * l + 44] = _vec8(inp["b_fdw"][l])
        for k in range(3):
            o = V_WFDW + (l * 3 + k) * 44
            vecs[:, o:o + 44] = _vec8(inp["w_fdw"][l, k])
    vecs[:, V_BPW1:V_BPW1 + 16] = _vec8(inp["b_pw1"][0])
    vecs[:, V_BCDW:V_BCDW + 8] = _vec8(inp["b_cdw"][0])
    vecs[:, V_LNG:V_LNG + 8] = _vec8(inp["ln_g"][0])
    vecs[:, V_LNB:V_LNB + 8] = _vec8(inp["ln_b"][0])
    vecs[:, V_BPW2:V_BPW2 + 8] = _vec8(inp["b_pw2"][0])
    for k in range(31):
        vecs[:, V_WCDW + 8 * k:V_WCDW + 8 * k + 8] = _vec8(inp["w_cdw"][0, k])
    vecs[:, V_QK] = np.tile(np.asarray(inp["q_gain"][0], np.float32), 2)
    vecs[:, V_QK + 1] = np.tile(np.asarray(inp["k_gain"][0], np.float32), 2)
    sh["vecs"] = vecs
    consts = np.zeros((128, 640), np.float32)
    consts[64, 512:576] = 1.0
    consts[:, 0:128] = np.eye(128, dtype=np.float32)
    consts[:, 128:256] = 1.0
    for hh in range(2):
        consts[hh * 64:(hh + 1) * 64, 256 + hh * 64:256 + (hh + 1) * 64] = 1.0
    for dst in range(128):
        within = dst % 32
        partner = dst + 16 if within < 16 else dst - 16
        consts[partner, 384 + dst] = 1.0
    sh["consts"] = consts
    cc = np.stack([np.asarray(inp["c"][0], np.float32), np.asarray(inp["c_ctx"], np.float32)], 0)
    sh["cT"] = np.ascontiguousarray(cc.reshape(2, 8, 128).transpose(2, 1, 0).reshape(128, 16))
    sh["ctx"] = np.ascontiguousarray(inp["ctx"][0], dtype=np.float32)
    sh["w_in"] = np.ascontiguousarray(inp["w_in_hyb"][0], dtype=np.float32)
    sh["w_out"] = np.ascontiguousarray(inp["w_out_hyb"][0], dtype=np.float32)
    sh["w_pw1"] = np.ascontiguousarray(inp["w_pw1"][0], dtype=np.float32)
    sh["w_pw2"] = np.ascontiguousarray(inp["w_pw2"][0], dtype=np.float32)
    sh["w_up"] = np.ascontiguousarray(inp["w_up"], dtype=np.float32)
    sh["w_down"] = np.ascontiguousarray(inp["w_down"], dtype=np.float32)
    bf = ml_dtypes.bfloat16
    cidx = np.arange(64)[:, None].astype(np.float64)
    midx = np.arange(64)[None, :].astype(np.float64)
    ang = 2 * np.pi * cidx * midx / 64.0
    sh["e64"] = np.concatenate([np.cos(ang), -np.sin(ang)], 1).astype(bf)
    a = np.arange(128)[:, None].astype(np.float64)
    b = np.arange(128)[None, :].astype(np.float64)
    ang = 2 * np.pi * a * b / 128.0
    C, S = np.cos(ang), np.sin(ang)
    sh["cs12"] = np.concatenate([C, -S, S, C], 1).astype(bf)
    ang = 2 * np.pi * a * b / 16384.0
    sh["tw"] = np.concatenate([np.tile(np.cos(ang), (1, 4)), np.tile(np.sin(ang), (1, 4))], 1).astype(np.float32)
    return sh


def _prep_core(inp, i):
    pc = {}
    x = inp["x"][0]
    xi = np.zeros((NTP, D), np.float32)
    g0 = OWN * i - HALO
    lo, hi = max(g0, 0), min(g0 + NT, SEQ)
    xi[lo - g0:hi - g0] = x[lo:hi]
    pc["x"] = xi
    pos = np.clip(np.arange(g0, g0 + NT), 0, SEQ - 1)
    row = (pos // 64).astype(np.float32)
    col = (pos % 64).astype(np.float32)
    inv_freq = (1.0 / (np.float32(10000.0) ** (np.arange(0, 32, 2, dtype=np.float32) / np.float32(32)))).astype(np.float32)
    cos = np.zeros((64, NT), np.float32)
    sin = np.zeros((64, NT), np.float32)
    for d in range(64):
        p = row if d < 32 else col
        within = d % 32
        f = within % 16
        angv = (p * inv_freq[f]).astype(np.float32)
        cos[d] = np.cos(angv)
        sin[d] = np.sin(angv) * (-1.0 if within < 16 else 1.0)
    pc["rope"] = np.concatenate([np.tile(cos, (2, 1)), np.tile(sin, (2, 1))], 1).astype(np.float32)
    m = np.ones((128, 2), np.float32)
    if i == 0:
        m[:, 0] = 0.0
    if i == NCORE - 1:
        m[:, 1] = 0.0
    pc["masks"] = m
    pc["w_ada"] = np.ascontiguousarray(inp["w_ada"][:, :, 768 * i:768 * (i + 1)], dtype=np.float32)
    pc["wf"] = np.ascontiguousarray(inp["w_in_hyb"][0][:, 768 + 64 * i:768 + 64 * (i + 1)], dtype=np.float32)
    l1 = np.arange(128)[:, None].astype(np.float64)
    k1 = ((16 * i - 1 + np.arange(18)) % 128)[None, :].astype(np.float64)
    ang = 2 * np.pi * l1 * k1 / 128.0
    pc["cs3"] = np.concatenate([np.cos(ang), np.sin(ang)], 1).astype(ml_dtypes.bfloat16)
    return pc


_NC_CACHE = {}
_LAST = {}


def kernel(**inputs):
    inp = {k: np.asarray(v) for k, v in inputs.items()}
    if "nc" not in _NC_CACHE:
        _NC_CACHE["nc"] = build_program()
    nc = _NC_CACHE["nc"]
    sh = _prep_shared(inp)
    in_maps = []
    for i in range(NCORE):
        m = dict(sh)
        m.update(_prep_core(inp, i))
        in_maps.append(m)
    res = run_bass_kernel_spmd(nc, in_maps, core_ids=list(range(NCORE)))
    out = np.concatenate([np.asarray(r["out"], np.float32) for r in res.results], 0)
    return out.reshape(1, SEQ, D)
```

```python
import numpy as np
import ml_dtypes
from contextlib import ExitStack
import concourse.bass as bass
import concourse.mybir as mybir
from concourse.bass_utils import run_bass_kernel_spmd

F32 = mybir.dt.float32
BF16 = mybir.dt.bfloat16
ALU = mybir.AluOpType
AF = mybir.ActivationFunctionType

NCORE = 8
D = 1024
SEQ = 16384
OWN = 2048
HALO = 17
NT = OWN + 2 * HALO
NTP = 17 * 128
CTX = 256
TT = [(0, 416), (416, 832), (832, 1248), (1248, 1664), (1664, 2082)]
FFN = 2816
NJ = FFN // 128
EPS = 1e-6
LN_EPS = 1e-5
NKC = 130
_LAST = {}
ENGS = ["tensor", "vector", "scalar", "gpsimd", "sync"]

V_GMIX = 0; V_GFFN = 16; V_BPW1 = 32; V_BCDW = 48; V_LNG = 56; V_LNB = 64; V_BPW2 = 72; V_WCDW = 80
V_BADA = V_WCDW + 31 * 8
V_BFDW = V_BADA + 96
V_WFDW = V_BFDW + 88
V_QK = V_WFDW + 264
NVEC = V_QK + 2


class Tile:
    __slots__ = ("ap", "w", "r")

    def __init__(self, ap):
        self.ap = ap
        self.w = None
        self.r = {}


class KB:
    def __init__(self, nc, stack):
        self.nc = nc
        self.stack = stack
        self.prog = {e: [] for e in ENGS}
        self.semobj = {}
        self.cur = {}
        self.known = {e: {} for e in ENGS}
        self.pending = {e: ([], []) for e in ENGS}
        self.nsem = 0
        for e in ENGS:
            self.cur[e] = [self.new_sem("e_" + e), 0]
        self.dmapool = {q: [[self.new_sem("d_%s%d" % (q, k)), 0] for k in range(n)]
                        for q, n in (("sync", 12), ("gpsimd", 8), ("scalar", 8))}
        self.dmaidx = {"sync": 0, "gpsimd": 0, "scalar": 0}
        self.done_sems = {}

    def new_sem(self, name):
        name = "%s_%d" % (name, self.nsem)
        self.nsem += 1
        self.semobj[name] = self.stack.enter_context(self.nc.semaphore(name))
        return name

    def _waits(self, eng, reads, writes, extra=()):
        need = {}

        def add(s, v):
            if need.get(s, 0) < v:
                need[s] = v
        for t in reads:
            if t.w is not None:
                add(*t.w)
        for t in writes:
            if t.w is not None:
                add(*t.w)
            for s, v in t.r.items():
                add(s, v)
        for s, v in extra:
            add(s, v)
        kn = self.known[eng]
        own = self.cur[eng][0]
        waits = []
        for s, v in need.items():
            if kn.get(s, 0) >= v:
                continue
            if eng == "tensor" and s == own:
                continue
            kn[s] = v
            waits.append((s, v))
        return waits

    def _register(self, ev, reads, writes):
        s, v = ev
        for t in reads:
            if t.r.get(s, 0) < v:
                t.r[s] = v
        for t in writes:
            t.w = ev
            t.r = {}

    def op(self, eng, fn, reads=(), writes=(), sig=True):
        waits = self._waits(eng, reads, writes)
        pr, pw = self.pending[eng]
        if sig:
            c = self.cur[eng]
            if c[1] >= 30000:
                self.done_sems[c[0]] = c[1]
                c[0] = self.new_sem("e_" + eng)
                c[1] = 0
            c[1] += 1
            ev = (c[0], c[1])
            self._register(ev, list(reads) + pr, list(writes) + pw)
            self.pending[eng] = ([], [])
            self.prog[eng].append((waits, fn, ev, 1))
        else:
            pr.extend(reads)
            pw.extend(writes)
            self.prog[eng].append((waits, fn, None, 0))

    def dma(self, q, out_ap, in_ap, reads=(), writes=()):
        pool = self.dmapool[q]
        k = self.dmaidx[q] % len(pool)
        self.dmaidx[q] += 1
        sname, cnt = pool[k]
        extra = [(sname, cnt)] if cnt > 0 else []
        waits = self._waits(q, reads, writes, extra)
        pool[k][1] = cnt + 16
        ev = (sname, cnt + 16)
        self._register(ev, reads, writes)
        self.prog[q].append((waits, lambda e: e.dma_start(out=out_ap, in_=in_ap), ev, 16))

    def allgather(self, in_t, out_t, in_handle, out_handle):
        s = self.new_sem("cc")
        waits = self._waits("gpsimd", [in_t], [out_t])
        ev = (s, 1)
        self._register(ev, [in_t], [out_t])

        def fn(e):
            return e.collective_compute(
                "AllGather", ALU.bypass, replica_groups=[list(range(NCORE))],
                ins=[in_handle.ap().opt()], outs=[out_handle.ap().opt()])
        self.prog["gpsimd"].append((waits, fn, ev, None))

    def barrier(self):
        evs = dict(self.done_sems)
        for e in ENGS:
            c = self.cur[e]
            if c[1] > 0:
                evs[c[0]] = c[1]
        for q, pool in self.dmapool.items():
            for s, c in pool:
                if c > 0:
                    evs[s] = c
        for e in ENGS:
            kn = self.known[e]
            waits = []
            for s, v in evs.items():
                if kn.get(s, 0) < v:
                    kn[s] = v
                    waits.append((s, v))
            self.prog[e].append((waits, None, None, 0))

    def emit_all(self, final_waits):
        kn = self.known["sync"]
        waits = [(s, v) for s, v in final_waits if kn.get(s, 0) < v]
        self.prog["sync"].append((waits, None, None, 0))
        with self.nc.Block() as block:
            for eng in ENGS:
                getattr(block, eng)(self._runner(eng))

    def _runner(self, eng):
        prog = self.prog[eng]
        semobj = self.semobj

        def run(e):
            for waits, fn, ev, inc in prog:
                for s, v in waits:
                    e.wait_ge(semobj[s], v)
                if fn is None:
                    continue
                ins = fn(e)
                if ev is not None:
                    if inc is None:
                        ins.then_inc(semobj[ev[0]])
                    else:
                        ins.then_inc(semobj[ev[0]], inc)
        return run


class Mem:
    def __init__(self, big, base, limit, kb):
        self.big = big
        self.base = base
        self.top = base
        self.limit = limit
        self.hi = limit
        self.kb = kb

    def mark(self):
        return self.top

    def release(self, m):
        self.top = m
        self.kb.barrier()

    def f32(self, cols, parts=128):
        off = self.top
        self.top += cols
        assert self.top <= self.hi, ("SBUF overflow", self.top, self.hi)
        return self.big[0:parts, off:off + cols]

    def bf16(self, cols, parts=128):
        c32 = (cols + 1) // 2
        off = self.top
        self.top += c32
        assert self.top <= self.hi, ("SBUF overflow", self.top, self.hi)
        return self.big[0:parts, off:off + c32].bitcast(BF16)[:, 0:cols]

    def bf16_top(self, cols, parts=128):
        c32 = (cols + 1) // 2
        self.hi -= c32
        assert self.top <= self.hi, ("SBUF overflow", self.top, self.hi)
        return self.big[0:parts, self.hi:self.hi + c32].bitcast(BF16)[:, 0:cols]

    def release_top(self):
        self.hi = self.limit
        self.kb.barrier()


def build_program(debug=None):
    nc = bass.Bass("TRN2", target_bir_lowering=False)
    stack = ExitStack()
    with stack:
        _build(nc, stack, debug)
    return nc


def _build(nc, stack, debug):
    kb = KB(nc, stack)
    _LAST['kb'] = kb

    def din(name, shape, dt=F32):
        return nc.dram_tensor(name, list(shape), dt, kind="ExternalInput")

    def dtmp(name, shape, dt):
        return nc.dram_tensor(name, list(shape), dt)

    x_d = din("x", [NTP, D])
    ctx_d = din("ctx", [CTX, D])
    cT_d = din("cT", [128, 16])
    wada_d = din("w_ada", [2, D, 768])
    vecs_d = din("vecs", [128, NVEC])
    consts_d = din("consts", [128, 5 * 128])
    rope_d = din("rope", [128, 2 * NT])
    masks_d = din("masks", [128, 2])
    win_d = din("w_in", [D, 1280])
    wf_d = din("wf", [D, 64])
    wout_d = din("w_out", [D, D])
    wpw1_d = din("w_pw1", [D, 2 * D])
    wpw2_d = din("w_pw2", [D, D])
    wup_d = din("w_up", [2, D, 2 * FFN])
    wdown_d = din("w_down", [2, FFN, D])
    e64_d = din("e64", [64, 128], BF16)
    cs12_d = din("cs12", [128, 512], BF16)
    tw_d = din("tw", [128, 1024])
    cs3_d = din("cs3", [128, 2 * 18], BF16)
    out_d = nc.dram_tensor("out", [OWN, D], F32, kind="ExternalOutput")

    k_in = dtmp("k_in", [128, OWN], BF16)
    k_all = dtmp("k_all", [NCORE * 128, OWN], BF16)
    v_in = dtmp("v_in", [128, 16 * 130], BF16)
    v_all = dtmp("v_all", [NCORE * 128, 16 * 130], BF16)
    h_in = dtmp("h_in", [D, OWN], BF16)
    h_all = dtmp("h_all", [NCORE * D, OWN], BF16)
    tp_in = dtmp("tp_in", [128, 16384], BF16)
    tp_all = dtmp("tp_all", [NCORE * 128, 16384], BF16)
    xsp = dtmp("xsp", [128, 8 * NT], F32)
    t_k_in, t_k_all, t_v_in, t_v_all = Tile(None), Tile(None), Tile(None), Tile(None)
    t_h_in, t_h_all, t_tp_in, t_tp_all, t_xsp = Tile(None), Tile(None), Tile(None), Tile(None), Tile(None)
    t_out = Tile(None)
    ag_in = dtmp("ag_in", [128, 24], F32)
    ag_out = dtmp("ag_out", [NCORE * 128, 24], F32)
    t_ag_in, t_ag_out = Tile(None), Tile(None)

    dbg = {}

    def dbg_out(name, shape, dt=F32):
        dbg[name] = nc.dram_tensor("dbg_" + name, list(shape), dt, kind="ExternalOutput")
        return dbg[name]

    TOTAL = 53200
    PERS = 6000
    XR = 8 * NT
    big = stack.enter_context(nc.sbuf_tensor("big", [128, TOTAL], F32))
    psum = stack.enter_context(nc.psum_tensor("psum", [128, 4096], F32))
    pmem = Mem(big, 0, PERS, kb)
    memx = Mem(big, PERS, PERS + XR, kb)
    mem = Mem(big, PERS + XR, TOTAL, kb)
    PS = [Tile(psum[:, 512 * b:512 * (b + 1)]) for b in range(8)]
    psrr = [0]

    def nextps():
        b = psrr[0] % 8
        psrr[0] += 1
        return PS[b]

    def taps():
        return [(s, c) for q in kb.dmapool for s, c in kb.dmapool[q] if c > 0]

    consts = Tile(pmem.f32(5 * 128))
    ident = consts.ap[:, 0:128]
    ones = consts.ap[:, 128:256]
    blk = consts.ap[:, 256:384]
    pm = consts.ap[:, 384:512]
    sel = consts.ap[:, 512:576]
    vecs = Tile(pmem.f32(NVEC))
    rope = Tile(pmem.f32(2 * NT))
    masks = Tile(pmem.f32(2))
    mod = Tile(pmem.f32(2 * 48 * 2))
    der = Tile(pmem.f32(64))
    epsb = Tile(pmem.f32(8))
    cbf = Tile(pmem.bf16(256))
    kb.dma("sync", consts.ap, consts_d.ap(), writes=[consts])
    kb.dma("sync", vecs.ap, vecs_d.ap(), writes=[vecs])
    kb.dma("sync", rope.ap, rope_d.ap(), writes=[rope])
    kb.dma("sync", masks.ap, masks_d.ap(), writes=[masks])
    kb.op("gpsimd", lambda e: e.tensor_copy(out=cbf.ap, in_=consts.ap[:, 0:256]), [consts], [cbf])
    ident_bf = cbf.ap[:, 0:128]
    ones_bf = cbf.ap[:, 128:256]
    kb.op("vector", lambda e: e.memset(epsb.ap[:, 0:1], EPS), [], [epsb])
    kb.op("vector", lambda e: e.memset(epsb.ap[:, 1:2], LN_EPS), [], [epsb])
    kb.op("vector", lambda e: e.tensor_scalar(out=epsb.ap[:, 2:3], in0=vecs.ap[:, V_QK:V_QK + 1], scalar1=0.125,
                                              scalar2=None, op0=ALU.mult), [vecs], [epsb])
    kb.op("vector", lambda e: e.tensor_copy(out=epsb.ap[:, 3:4], in_=vecs.ap[:, V_QK + 1:V_QK + 2]), [vecs], [epsb])

    def modv(l, w, c, j=0):
        o = ((l * 48) + w * 8 + c) * 2 + j
        return mod.ap[:, o:o + 1]

    def vcol(base, c):
        return vecs.ap[:, base + c:base + c + 1]

    xT_ap = memx.f32(8 * NT)
    XT = [[Tile(xT_ap[:, c * NT + a:c * NT + b]) for (a, b) in TT] for c in range(8)]
    ALLXT = [t for row in XT for t in row]

    def xt_cols(c, a, b):
        return xT_ap[:, c * NT + a:c * NT + b]

    m0 = mem.mark()
    cT = Tile(mem.f32(16))
    kb.dma("sync", cT.ap, cT_d.ap(), writes=[cT])
    sc = Tile(mem.f32(16))
    kb.op("scalar", lambda e: e.activation(out=sc.ap, in_=cT.ap, func=AF.Sigmoid), [cT], [sc])
    kb.op("vector", lambda e: e.tensor_tensor(out=sc.ap, in0=sc.ap, in1=cT.ap, op=ALU.mult), [sc, cT], [sc])
    wst = [Tile(mem.f32(8 * 768)) for _ in range(2)]
    moc = Tile(mem.f32(24))
    modraw = Tile(mem.f32(8 * 24))
    for l in range(2):
        w = wst[l]
        kb.dma("sync", w.ap.rearrange("p (kc n) -> p kc n", kc=8),
               wada_d.ap()[l].rearrange("(kc p) n -> p kc n", p=128), writes=[w])
        pst = nextps()
        for cc in range(6):
            for kc in range(8):
                kb.op("tensor", (lambda e, w=w, pst=pst, cc=cc, kc=kc: e.matmul(
                    pst.ap[:, cc * 2:cc * 2 + 2], lhsT=w.ap[:, kc * 768 + cc * 128:kc * 768 + cc * 128 + 128],
                    rhs=sc.ap[:, kc * 2:kc * 2 + 2], start=(kc == 0), stop=(kc == 7))),
                    [w, sc], [pst], sig=(cc == 5 and kc == 7))
        kb.op("vector", (lambda e, pst=pst, l=l: e.tensor_copy(out=moc.ap[:, l * 12:l * 12 + 12], in_=pst.ap[:, 0:12])),
              [pst], [moc])
    kb.dma("sync", ag_in.ap(), moc.ap, reads=[moc], writes=[t_ag_in])
    kb.allgather(t_ag_in, t_ag_out, ag_in, ag_out)
    kb.dma("sync", modraw.ap.rearrange("p (r f) -> p r f", r=8), ag_out.ap().rearrange("(r p) f -> p r f", p=128),
           reads=[t_ag_out], writes=[modraw])
    for l in range(2):
        for j in range(2):
            kb.op("vector", (lambda e, l=l, j=j: e.tensor_tensor(
                out=mod.ap[:, l * 96 + j:l * 96 + j + 95:2].rearrange("p (r c) -> p r c", r=8),
                in0=modraw.ap.rearrange("p (r f) -> p r f", r=8)[:, :, l * 12 + j:l * 12 + j + 11:2],
                in1=vecs.ap[:, V_BADA + l * 48:V_BADA + l * 48 + 48].rearrange("p (r c) -> p r c", r=8),
                op=ALU.add)), [modraw, vecs], [mod])
    mem.release(m0)

    def derv(l, kind, c):
        o = (l * 3 + kind) * 8 + c
        return der.ap[:, o:o + 1]

    for l in range(2):
        for kind, (w, j, gb) in enumerate([(1, 0, V_GMIX), (1, 1, V_GMIX), (4, 0, V_GFFN)]):
            o = (l * 3 + kind) * 8
            mo = (l * 48 + w * 8) * 2 + j
            kb.op("vector", (lambda e, o=o, mo=mo, gb=gb, l=l: e.scalar_tensor_tensor(
                out=der.ap[:, o:o + 8], in0=mod.ap[:, mo:mo + 15:2], scalar=1.0,
                in1=vecs.ap[:, gb + l * 8:gb + l * 8 + 8], op0=ALU.add, op1=ALU.mult)),
                [mod, vecs], [der])
    kb.op("vector", lambda e: e.tensor_tensor(
        out=der.ap[:, 48:56], in0=vecs.ap[:, V_BPW2:V_BPW2 + 8],
        in1=mod.ap[:, (48 + 16) * 2:(48 + 16) * 2 + 15:2], op=ALU.mult), [mod, vecs], [der])

    def norm_to_hT(src_cols, ntiles, G, S, hT_tiles, hT_cols):
        mm = mem.mark()
        sq = [Tile(mem.f32(418)) for _ in range(4)]
        rs = [Tile(mem.f32(418)) for _ in range(2)]
        tm = [Tile(mem.f32(418)) for _ in range(3)]
        cnt = 0
        for ti, (a, b, srct) in enumerate(ntiles):
            n = b - a
            pst = nextps()
            for c in range(8):
                s = sq[cnt % 4]
                cnt += 1
                kb.op("gpsimd" if c % 2 else "vector", (lambda e, s=s, c=c, a=a, b=b, n=n: e.tensor_tensor(
                    out=s.ap[:, 0:n], in0=src_cols(c, a, b), in1=src_cols(c, a, b), op=ALU.mult)), [srct[c]], [s])
                kb.op("tensor", (lambda e, s=s, c=c, pst=pst, n=n: e.matmul(
                    pst.ap[:, 0:n], lhsT=ones, rhs=s.ap[:, 0:n], start=(c == 0), stop=(c == 7))),
                    [s, consts], [pst], sig=True)
            r = rs[ti % 2]
            kb.op("scalar", (lambda e, r=r, pst=pst, n=n: e.activation(
                out=r.ap[:, 0:n], in_=pst.ap[:, 0:n], func=AF.Sqrt, scale=1.0 / D, bias=epsb.ap[:, 0:1])),
                [pst, epsb], [r])
            kb.op("vector", (lambda e, r=r, n=n: e.reciprocal(out=r.ap[:, 0:n], in_=r.ap[:, 0:n])), [r], [r])
            for c in range(8):
                t = tm[c % 3]
                kb.op("vector", (lambda e, t=t, c=c, a=a, b=b, n=n, r=r: e.tensor_tensor(
                    out=t.ap[:, 0:n], in0=src_cols(c, a, b), in1=r.ap[:, 0:n], op=ALU.mult)),
                    [srct[c], r], [t])
                kb.op("scalar", (lambda e, t=t, c=c, a=a, b=b, n=n: e.activation(
                    out=hT_cols(c, a, b), in_=t.ap[:, 0:n], func=AF.Identity, scale=G(c), bias=S(c))),
                    [t, mod, der], [hT_tiles[c][ti]])
        mem.release(mm)

    def load_transposed(src_d, nrows_total, dst_ap, dst_stride, dst_tiles_fn):
        mm = mem.mark()
        xin = [Tile(mem.f32(D)) for _ in range(2)]
        nt = (nrows_total + 127) // 128
        for t in range(nt):
            xi = xin[t % 2]
            rows = min(128, nrows_total - t * 128)
            kb.dma("gpsimd", xi.ap[0:rows, :], src_d.ap()[t * 128:t * 128 + rows, :], writes=[xi])
            a0 = t * 128
            for hlf in range(2):
                pst = nextps()
                for cc in range(4):
                    c = hlf * 4 + cc
                    kb.op("tensor", (lambda e, xi=xi, pst=pst, cc=cc, c=c, rows=rows: e.transpose(
                        out=pst.ap[:, cc * 128:cc * 128 + rows], in_=xi.ap[0:rows, c * 128:(c + 1) * 128],
                        identity=ident[0:rows, 0:rows])), [xi, consts], [pst], sig=(cc == 3))
                wt = dst_tiles_fn(hlf, a0, a0 + rows)
                kb.op("scalar", (lambda e, pst=pst, hlf=hlf, a0=a0, rows=rows: e.activation(
                    out=bass.AP(dst_ap.tensor, dst_ap.offset + hlf * 4 * dst_stride + a0,
                                [list(dst_ap.ap[0]), [dst_stride, 4], [1, rows]]),
                    in_=bass.AP(pst.ap.tensor, pst.ap.offset, [list(pst.ap.ap[0]), [128, 4], [1, rows]]),
                    func=AF.Identity)), [pst], wt)
        mem.release(mm)

    def xt_tiles_touch(hlf, a0, b0):
        return [XT[hlf * 4 + cc][ti] for cc in range(4) for ti, (a, b) in enumerate(TT) if a < b0 and b > a0]

    def tt_list(tiles):
        return [(a, b, [tiles[c][ti] for c in range(8)]) for ti, (a, b) in enumerate(TT)]

    def cast_load(dst_tile, dst_ap3, src_ap3, stage_tiles, idx, stage_view):
        st = stage_tiles[idx % len(stage_tiles)]
        kb.dma("sync", stage_view(st), src_ap3, writes=[st])
        kb.op("gpsimd", lambda e: e.tensor_copy(out=dst_ap3, in_=stage_view(st)), [st], [dst_tile])

    load_transposed(x_d, NT, xT_ap, NT, xt_tiles_touch)
    if debug == "s1":
        o = dbg_out("xT", [128, 8 * NT])
        kb.dma("sync", o.ap(), xT_ap, reads=ALLXT, writes=[t_out])
        o2 = dbg_out("mod", [128, 192])
        kb.dma("sync", o2.ap(), mod.ap, reads=[mod], writes=[t_out])
        kb.emit_all(taps())
        return
    hT_ap = mem.bf16(8 * NT)
    HT = [[Tile(hT_ap[:, c * NT + a:c * NT + b]) for (a, b) in TT] for c in range(8)]
    ALLHT = [t for row in HT for t in row]

    def ht_cols(c, a, b):
        return hT_ap[:, c * NT + a:c * NT + b]
    norm_to_hT(xt_cols, tt_list(XT), lambda c: derv(0, 0, c), lambda c: modv(0, 0, c, 0), HT, ht_cols)
    for c in range(8):
        kb.dma("sync", h_in.ap()[c * 128:(c + 1) * 128, :], hT_ap[:, c * NT + HALO:c * NT + HALO + OWN],
               reads=HT[c], writes=[t_h_in])
    kb.barrier()
    memx.top = memx.base

    QZ = Tile(memx.bf16(8 * NT))
    kb.op("gpsimd", lambda e: e.memset(QZ.ap, 0.0), [], [QZ])
    wfb = Tile(memx.bf16(8 * 64))
    kcb = Tile(memx.bf16(CTX))
    vctx = Tile(memx.bf16(2 * 130))
    mxkeep = memx.mark()
    cx_ap = memx.f32(8 * CTX)
    CXT = [[Tile(cx_ap[:, c * CTX:(c + 1) * CTX])] for c in range(8)]
    load_transposed(ctx_d, CTX, cx_ap, CTX, lambda hlf, a0, b0: [CXT[hlf * 4 + cc][0] for cc in range(4)])
    hc_ap = memx.bf16(8 * CTX)
    HCT = [[Tile(hc_ap[:, c * CTX:(c + 1) * CTX])] for c in range(8)]
    norm_to_hT(lambda c, a, b: cx_ap[:, c * CTX + a:c * CTX + b], [(0, CTX, [CXT[c][0] for c in range(8)])],
               lambda c: derv(0, 1, c), lambda c: modv(0, 0, c, 1), HCT,
               lambda c, a, b: hc_ap[:, c * CTX + a:c * CTX + b])

    win = Tile(memx.bf16(8 * 768))
    mst = mem.mark()
    stg = [Tile(mem.f32(768)) for _ in range(2)]
    for kc in range(8):
        st = stg[kc % 2]
        kb.dma("sync", st.ap, win_d.ap()[kc * 128:(kc + 1) * 128, 0:768], writes=[st])
        kb.op("gpsimd", (lambda e, st=st, kc=kc: e.tensor_copy(
            out=win.ap[:, kc * 768:kc * 768 + 512].rearrange("p (c g d) -> p c g d", c=4, g=2),
            in_=st.ap[:, 0:512].rearrange("p (g c d) -> p c g d", g=2, c=4))), [st], [win])
        kb.op("gpsimd", (lambda e, st=st, kc=kc: e.tensor_copy(
            out=win.ap[:, kc * 768 + 512:kc * 768 + 768], in_=st.ap[:, 512:768])), [st], [win])
    st = stg[0]
    kb.dma("sync", st.ap[:, 0:512].rearrange("p (kc n) -> p kc n", kc=8),
           wf_d.ap().rearrange("(kc p) n -> p kc n", p=128), writes=[st])
    kb.op("gpsimd", lambda e, st0=stg[0]: e.tensor_copy(out=wfb.ap, in_=st0.ap[:, 0:512]), [stg[0]], [wfb])
    mem.release(mst)
    kb.allgather(t_h_in, t_h_all, h_in, h_all)

    KTl = Tile(mem.bf16(NT))
    vloc = Tile(mem.bf16(16 * 130))
    kb.op("gpsimd", lambda e: e.memset(vloc.ap, 1.0), [], [vloc])
    kb.op("gpsimd", lambda e: e.memset(vctx.ap, 1.0), [], [vctx])
    NB = 3
    qraw = [Tile(mem.f32(418)) for _ in range(NB)]
    sqt = [Tile(mem.f32(418)) for _ in range(NB)]
    rst = [Tile(mem.f32(418)) for _ in range(NB)]
    qgt = [Tile(mem.f32(418)) for _ in range(NB)]
    t1t = [Tile(mem.f32(418)) for _ in range(NB)]
    t2t = [Tile(mem.f32(418)) for _ in range(NB)]
    it = [0]

    def qk_block(wcol, rhs_fn, rtiles, n, gaincol, ropecols, out_fns):
        i = it[0] % NB
        it[0] += 1
        ps = nextps()
        for kc in range(8):
            kb.op("tensor", (lambda e, kc=kc, ps=ps: e.matmul(
                ps.ap[:, 0:n], lhsT=win.ap[:, kc * 768 + wcol:kc * 768 + wcol + 128], rhs=rhs_fn(kc),
                start=(kc == 0), stop=(kc == 7))), [win] + rtiles, [ps], sig=(kc == 7))
        qr, sq, r, qg, t1, t2 = qraw[i], sqt[i], rst[i], qgt[i], t1t[i], t2t[i]
        kb.op("scalar", lambda e: e.activation(out=qr.ap[:, 0:n], in_=ps.ap[:, 0:n], func=AF.Identity), [ps], [qr])
        kb.op("scalar", lambda e: e.activation(out=sq.ap[:, 0:n], in_=ps.ap[:, 0:n], func=AF.Square), [ps], [sq])
        ps2 = nextps()
        kb.op("tensor", lambda e: e.matmul(ps2.ap[:, 0:n], lhsT=blk, rhs=sq.ap[:, 0:n], start=True, stop=True),
              [sq, consts], [ps2])
        kb.op("scalar", lambda e: e.activation(out=r.ap[:, 0:n], in_=ps2.ap[:, 0:n], func=AF.Sqrt, scale=1.0 / 64,
                                               bias=epsb.ap[:, 0:1]), [ps2, epsb], [r])
        kb.op("vector", lambda e: e.reciprocal(out=r.ap[:, 0:n], in_=r.ap[:, 0:n]), [r], [r])
        kb.op("vector", lambda e: e.scalar_tensor_tensor(out=qg.ap[:, 0:n], in0=qr.ap[:, 0:n], scalar=gaincol,
                                                         in1=r.ap[:, 0:n], op0=ALU.mult, op1=ALU.mult),
              [qr, r, epsb], [qg])
        if ropecols is None:
            for (oap, otile, p0, p1) in out_fns:
                kb.op("vector", (lambda e, oap=oap, p0=p0, p1=p1: e.tensor_copy(out=oap, in_=qg.ap[p0:p1, 0:n])),
                      [qg], [otile])
            return
        a, b = ropecols
        ps3 = nextps()
        kb.op("tensor", lambda e: e.matmul(ps3.ap[:, 0:n], lhsT=pm, rhs=qg.ap[:, 0:n], start=True, stop=True),
              [qg, consts], [ps3])
        kb.op("gpsimd", lambda e: e.tensor_tensor(out=t1.ap[:, 0:n], in0=qg.ap[:, 0:n], in1=rope.ap[:, a:b],
                                                  op=ALU.mult), [qg, rope], [t1])
        kb.op("vector", lambda e: e.tensor_tensor(out=t2.ap[:, 0:n], in0=ps3.ap[:, 0:n],
                                                  in1=rope.ap[:, NT + a:NT + b], op=ALU.mult), [ps3, rope], [t2])
        for (oap, otile, p0, p1) in out_fns:
            kb.op("vector", (lambda e, oap=oap, p0=p0, p1=p1: e.tensor_tensor(
                out=oap, in0=t1.ap[p0:p1, 0:n], in1=t2.ap[p0:p1, 0:n], op=ALU.add)), [t1, t2], [otile])

    for qc in range(5):
        for ti, (a, b) in enumerate(TT):
            n = b - a
            rhs_fn = (lambda kc, a=a, b=b: hT_ap[:, kc * NT + a:kc * NT + b])
            rt = [HT[kc][ti] for kc in range(8)]
            if qc < 4:
                outs = [(QZ.ap[0:64, qc * NT + a:qc * NT + b], QZ, 0, 64),
                        (QZ.ap[64:128, (4 + qc) * NT + a:(4 + qc) * NT + b], QZ, 64, 128)]
                qk_block(qc * 128, rhs_fn, rt, n, epsb.ap[:, 2:3], (a, b), outs)
            else:
                outs = [(KTl.ap[:, a:b], KTl, 0, 128)]
                qk_block(512, rhs_fn, rt, n, epsb.ap[:, 3:4], (a, b), outs)
    qk_block(512, lambda kc: hc_ap[:, kc * CTX:(kc + 1) * CTX], [HCT[kc][0] for kc in range(8)], CTX,
             epsb.ap[:, 3:4], None, [(kcb.ap[:, 0:CTX], kcb, 0, 128)])
    for t in range(18):
        ps = nextps()
        if t < 16:
            lt = (lambda kc, t=t: hT_ap[:, kc * NT + HALO + 128 * t:kc * NT + HALO + 128 * t + 128])
            rt = ALLHT
            dst, dtile = vloc.ap[:, t * 130:(t + 1) * 130], vloc
        else:
            lt = (lambda kc, t=t: hc_ap[:, kc * CTX + (t - 16) * 128:kc * CTX + (t - 16) * 128 + 128])
            rt = [HCT[kc][0] for kc in range(8)]
            dst, dtile = vctx.ap[:, (t - 16) * 130:(t - 15) * 130], vctx
        for kc in range(8):
            kb.op("tensor", (lambda e, kc=kc, ps=ps, lt=lt: e.matmul(
                ps.ap[:, 0:128], lhsT=lt(kc), rhs=win.ap[:, kc * 768 + 640:kc * 768 + 768],
                start=(kc == 0), stop=(kc == 7))), [win] + rt, [ps], sig=(kc == 7))
        kb.op("scalar", (lambda e, ps=ps, dst=dst: e.activation(
            out=dst.rearrange("p (g d) -> p g d", g=2)[:, :, 0:64],
            in_=ps.ap[:, 0:128].rearrange("p (g d) -> p g d", g=2), func=AF.Identity)), [ps], [dtile])
    kb.dma("sync", k_in.ap(), KTl.ap[:, HALO:HALO + OWN], reads=[KTl], writes=[t_k_in])
    kb.dma("sync", v_in.ap(), vloc.ap, reads=[vloc], writes=[t_v_in])
    kb.allgather(t_k_in, t_k_all, k_in, k_all)
    kb.allgather(t_v_in, t_v_all, v_in, v_all)
    if debug == "s3":
        o = dbg_out("QZ", [128, 8 * NT], BF16)
        kb.dma("sync", o.ap(), QZ.ap, reads=[QZ], writes=[t_out])
        o = dbg_out("KTl", [128, NT], BF16)
        kb.dma("sync", o.ap(), KTl.ap, reads=[KTl], writes=[t_out])
        o = dbg_out("vloc", [128, 16 * 130], BF16)
        kb.dma("sync", o.ap(), vloc.ap, reads=[vloc], writes=[t_out])
        o = dbg_out("kall", [NCORE * 128, OWN], BF16)
        kb.dma("sync", o.ap(), k_all.ap(), reads=[t_k_all], writes=[t_out])
        o = dbg_out("hT", [128, 8 * NT], BF16)
        kb.dma("sync", o.ap(), hT_ap, reads=ALLHT, writes=[t_out])
        kb.emit_all(taps())
        return
    memx.release(mxkeep)
    mem.top = mem.base

    mf = mem.mark()
    tabs = Tile(mem.bf16(128 + 512 + 36))
    e64 = tabs.ap[0:64, 0:128]
    cs12 = tabs.ap[:, 128:640]
    cs3 = tabs.ap[:, 640:676]
    kb.dma("sync", e64, e64_d.ap(), writes=[tabs])
    kb.dma("sync", cs12, cs12_d.ap(), writes=[tabs])
    kb.dma("sync", cs3, cs3_d.ap(), writes=[tabs])
    tw = Tile(mem.f32(1024))
    kb.dma("sync", tw.ap, tw_d.ap(), writes=[tw])
    fT = Tile(mem.bf16(16384))
    mh = mem.mark()
    hblk = [Tile(mem.bf16(8 * 1024)) for _ in range(2)]
    for bi in range(16):
        r, half = bi // 2, bi % 2
        hb = hblk[bi % 2]
        kb.dma("sync", hb.ap.rearrange("p (kc t) -> p kc t", kc=8),
               h_all.ap()[r * 1024:(r + 1) * 1024, half * 1024:(half + 1) * 1024].rearrange("(kc p) t -> p kc t", p=128),
               reads=[t_h_all], writes=[hb])
        for sub in range(2):
            ps = nextps()
            for kc in range(8):
                kb.op("tensor", (lambda e, kc=kc, ps=ps, hb=hb, sub=sub: e.matmul(
                    ps.ap[0:64, 0:512], lhsT=wfb.ap[:, kc * 64:(kc + 1) * 64],
                    rhs=hb.ap[:, kc * 1024 + sub * 512:kc * 1024 + sub * 512 + 512],
                    start=(kc == 0), stop=(kc == 7))), [wfb, hb], [ps], sig=(kc == 7))
            tok0 = r * 2048 + half * 1024 + sub * 512
            eng = "scalar" if sub == 0 else "vector"
            if eng == "scalar":
                kb.op("scalar", (lambda e, ps=ps, tok0=tok0: e.activation(
                    out=fT.ap[0:64, tok0:tok0 + 512], in_=ps.ap[0:64, 0:512], func=AF.Identity)), [ps], [fT])
            else:
                kb.op("vector", (lambda e, ps=ps, tok0=tok0: e.tensor_copy(
                    out=fT.ap[0:64, tok0:tok0 + 512], in_=ps.ap[0:64, 0:512])), [ps], [fT])
    mem.release(mh)
    Z = Tile(mem.bf16(16384))
    for q4 in range(32):
        ps = nextps()
        for qq in range(4):
            l1 = q4 * 4 + qq
            kb.op("tensor", (lambda e, ps=ps, qq=qq, l1=l1: e.matmul(
                ps.ap[:, qq * 128:(qq + 1) * 128], lhsT=fT.ap[0:64, l1:16384:128], rhs=e64,
                start=True, stop=True)), [fT, tabs], [ps], sig=(qq == 3))
        eng = "scalar" if q4 % 2 == 0 else "vector"
        zout = Z.ap.rearrange("p (x l) -> p l x", l=128)[:, q4 * 4:q4 * 4 + 4, :]
        zin = ps.ap[:, 0:512].rearrange("p (q x) -> p q x", q=4)
        if eng == "scalar":
            kb.op("scalar", (lambda e, zout=zout, zin=zin: e.activation(out=zout, in_=zin, func=AF.Identity)),
                  [ps], [Z])
        else:
            kb.op("vector", (lambda e, zout=zout, zin=zin: e.tensor_copy(out=zout, in_=zin)), [ps], [Z])
    p1t = [Tile(mem.f32(512)) for _ in range(2)]
    p2t = [Tile(mem.f32(512)) for _ in range(2)]
    tps = [Tile(mem.bf16(512)) for _ in range(2)]
    tp4 = tp_in.ap().rearrange("p (r m k) -> p r m k", r=2, m=64)
    for mb in range(32):
        ps = nextps()
        for mm_ in range(2):
            m = mb * 2 + mm_
            kb.op("tensor", (lambda e, ps=ps, mm_=mm_, m=m: e.matmul(
                ps.ap[:, mm_ * 256:(mm_ + 1) * 256], lhsT=Z.ap[:, m * 128:(m + 1) * 128], rhs=cs12[:, 0:256],
                start=True, stop=False)), [Z, tabs], [ps], sig=False)
            kb.op("tensor", (lambda e, ps=ps, mm_=mm_, m=m: e.matmul(
                ps.ap[:, mm_ * 256:(mm_ + 1) * 256], lhsT=Z.ap[:, (64 + m) * 128:(65 + m) * 128], rhs=cs12[:, 256:512],
                start=False, stop=True)), [Z, tabs], [ps], sig=(mm_ == 1))
        p1, p2, tp = p1t[mb % 2], p2t[mb % 2], tps[mb % 2]
        kb.op("vector", (lambda e, ps=ps, p1=p1: e.tensor_tensor(out=p1.ap, in0=ps.ap[:, 0:512], in1=tw.ap[:, 0:512],
                                                               op=ALU.mult)), [ps, tw], [p1])
        kb.op("vector", (lambda e, ps=ps, p2=p2: e.tensor_tensor(out=p2.ap, in0=ps.ap[:, 0:512], in1=tw.ap[:, 512:1024],
                                                               op=ALU.mult)), [ps, tw], [p2])
        p1v = p1.ap.rearrange("p (m r k) -> p m r k", m=2, r=2)
        p2v = p2.ap.rearrange("p (m r k) -> p m r k", m=2, r=2)
        tpv = tp.ap.rearrange("p (r m k) -> p r m k", r=2, m=2)
        kb.op("gpsimd", (lambda e, p1v=p1v, p2v=p2v, tpv=tpv: e.tensor_tensor(
            out=tpv[:, 0, :, :], in0=p1v[:, :, 0, :], in1=p2v[:, :, 1, :], op=ALU.add)), [p1, p2], [tp])
        kb.op("gpsimd", (lambda e, p1v=p1v, p2v=p2v, tpv=tpv: e.tensor_tensor(
            out=tpv[:, 1, :, :], in0=p1v[:, :, 1, :], in1=p2v[:, :, 0, :], op=ALU.subtract)), [p1, p2], [tp])
        kb.dma("sync", tp4[:, :, mb * 2:mb * 2 + 2, :], tpv, reads=[tp], writes=[t_tp_in])
    kb.allgather(t_tp_in, t_tp_all, tp_in, tp_all)
    mem.release(mf)
    aT = Tile(mem.bf16_top(8 * NT))

    KT = [Tile(mem.bf16(2048)) for _ in range(8)] + [Tile(mem.bf16(CTX))]
    kt_base = KT[0].ap
    KTall = bass.AP(kt_base.tensor, kt_base.offset, [list(kt_base.ap[0]), [1, 16384 + CTX]])
    VA = [Tile(mem.bf16(16 * 130)) for _ in range(8)] + [Tile(mem.bf16(2 * 130))]
    va_base = VA[0].ap
    VAall = bass.AP(va_base.tensor, va_base.offset, [list(va_base.ap[0]), [1, 130 * 130]])
    for r in range(8):
        kb.dma("gpsimd", KT[r].ap, k_all.ap()[r * 128:(r + 1) * 128, :], reads=[t_k_all], writes=[KT[r]])
        kb.dma("gpsimd", VA[r].ap, v_all.ap()[r * 128:(r + 1) * 128, :], reads=[t_v_all], writes=[VA[r]])
    kb.op("gpsimd", lambda e: e.tensor_copy(out=KT[8].ap, in_=kcb.ap), [kcb], [KT[8]])
    kb.op("gpsimd", lambda e: e.tensor_copy(out=VA[8].ap, in_=vctx.ap), [vctx], [VA[8]])
    PT = [Tile(mem.bf16(2 * 418)) for _ in range(4)]
    PSS = [Tile(psum[:, 1024 + 1024 * k:2048 + 1024 * k]) for k in range(3)]
    rr = Tile(mem.f32(418))
    kb.op("gpsimd", lambda e: e.memset(rr.ap, 0.0), [], [rr])
    osb = [Tile(mem.f32(418)) for _ in range(2)]
    for c in range(4):
        heads = (c, 4 + c)
        for ti, (a, b) in enumerate(TT):
            n = b - a
            for kc in range(NKC + 2):
                if kc < NKC:
                    ktile = KT[kc // 16] if kc < 128 else KT[8]
                    pss = PSS[kc % 3]
                    pt = PT[kc % 4]
                    for hi in range(2):
                        h = heads[hi]
                        kb.op("tensor", (lambda e, pss=pss, kc=kc, h=h, hi=hi, n=n, a=a, b=b: e.matmul(
                            pss.ap[:, hi * 512:hi * 512 + n], lhsT=KTall[:, kc * 128:(kc + 1) * 128],
                            rhs=QZ.ap[:, h * NT + a:h * NT + b], start=True, stop=True)), [ktile, QZ], [pss],
                            sig=(hi == 1))
                    kb.op("scalar", (lambda e, pss=pss, pt=pt, n=n: e.activation(
                        out=pt.ap.rearrange("p (h m) -> p h m", h=2)[:, :, 0:n],
                        in_=pss.ap.rearrange("p (h m) -> p h m", h=2)[:, :, 0:n], func=AF.Exp)), [pss], [pt])
                if kc >= 2:
                    k2 = kc - 2
                    vtile = VA[k2 // 16] if k2 < 128 else VA[8]
                    pt = PT[k2 % 4]
                    for hi in range(2):
                        kb.op("tensor", (lambda e, hi=hi, k2=k2, pt=pt, n=n: e.matmul(
                            PS[hi].ap[0:65, 0:n], lhsT=VAall[:, k2 * 130 + hi * 65:k2 * 130 + hi * 65 + 65],
                            rhs=pt.ap[:, hi * 418:hi * 418 + n], start=(k2 == 0), stop=(k2 == NKC - 1))),
                            [vtile, pt], [PS[hi]], sig=(k2 == NKC - 1 and hi == 1))
            for hi in range(2):
                h = heads[hi]
                ob = osb[hi]
                kb.op("vector", (lambda e, hi=hi, n=n: e.reciprocal(out=rr.ap[64:65, 0:n], in_=PS[hi].ap[64:65, 0:n])),
                      [PS[hi]], [rr])
                kb.op("tensor", (lambda e, n=n: e.matmul(PSS[0].ap[0:64, 0:n], lhsT=sel, rhs=rr.ap[:, 0:n],
                                                         start=True, stop=True)), [rr, consts], [PSS[0]])
                kb.op("scalar", (lambda e, hi=hi, ob=ob, n=n: e.activation(out=ob.ap[0:64, 0:n], in_=PS[hi].ap[0:64, 0:n],
                                                                      func=AF.Identity)), [PS[hi]], [ob])
                kb.op("vector", (lambda e, h=h, ob=ob, n=n, a=a, b=b: e.tensor_tensor(
                    out=aT.ap[0:64, h * NT + a:h * NT + b], in0=ob.ap[0:64, 0:n], in1=PSS[0].ap[0:64, 0:n],
                    op=ALU.mult)), [ob, PSS[0]], [aT])
    memx.top = memx.base
    mem.top = mem.base
    kb.barrier()
    YT = Tile(mem.bf16_top(8 * 2304))
    if debug == "s5":
        o = dbg_out("aT", [64, 8 * NT], BF16)
        kb.dma("sync", o.ap(), aT.ap[0:64, :], reads=[aT], writes=[t_out])
        kb.emit_all(taps())
        return

    tabs2 = Tile(mem.bf16(36))
    kb.dma("sync", tabs2.ap, cs3_d.ap(), writes=[tabs2])
    tpg = [Tile(memx.bf16(16384)) for _ in range(2)]
    for g in range(8):
        tb = tpg[g % 2]
        kb.dma("gpsimd", tb.ap, tp_all.ap()[g * 128:(g + 1) * 128, :], reads=[t_tp_all], writes=[tb])
        k2_0 = 0
        while k2_0 < 128:
            nk = min(28, 128 - k2_0)
            ps = nextps()
            for kk in range(nk):
                k2 = k2_0 + kk
                kb.op("tensor", (lambda e, ps=ps, kk=kk, k2=k2, tb=tb: e.matmul(
                    ps.ap[0:64, kk * 18:kk * 18 + 18], lhsT=tb.ap[:, k2:8192:128], rhs=tabs2.ap[:, 0:18],
                    start=True, stop=False)), [tb, tabs2], [ps], sig=False)
                kb.op("tensor", (lambda e, ps=ps, kk=kk, k2=k2, tb=tb: e.matmul(
                    ps.ap[0:64, kk * 18:kk * 18 + 18], lhsT=tb.ap[:, 8192 + k2:16384:128], rhs=tabs2.ap[:, 18:36],
                    start=False, stop=True)), [tb, tabs2], [ps], sig=(kk == nk - 1))
            yout = YT.ap[0:64, g * 2304:(g + 1) * 2304].rearrange("p (k1 k2) -> p k2 k1", k2=128)[:, k2_0:k2_0 + nk, :]
            yin = ps.ap[0:64, 0:nk * 18].rearrange("p (k a) -> p k a", a=18)
            kb.op("scalar", (lambda e, yout=yout, yin=yin: e.activation(out=yout, in_=yin, func=AF.Identity,
                                                                        scale=1.0 / 1024.0)), [ps], [YT])
            k2_0 += nk
    memx.top = memx.base
    kb.barrier()
    if debug == "s6":
        o = dbg_out("YT", [64, 8 * 2304], BF16)
        kb.dma("sync", o.ap(), YT.ap[0:64, :], reads=[YT], writes=[t_out])
        o = dbg_out("aT", [64, 8 * NT], BF16)
        kb.dma("sync", o.ap(), aT.ap[0:64, :], reads=[aT], writes=[t_out])
        kb.emit_all(taps())
        return

    xT_ap2 = memx.f32(8 * NT)
    load_transposed(x_d, NT, xT_ap, NT, xt_tiles_touch)
    mo_ = mem.mark()
    wo = Tile(mem.bf16(16 * 1024))
    for ch in range(16):
        kb.dma("gpsimd", wo.ap[0:64, ch * 1024:(ch + 1) * 1024], wout_d.ap()[ch * 64:(ch + 1) * 64, :], writes=[wo])
    for nchk in range(8):
        for ti, (a, b) in enumerate(TT):
            n = b - a
            ps = nextps()
            for ch in range(16):
                if ch < 8:
                    rhs = aT.ap[0:64, ch * NT + a:ch * NT + b]
                    rt = aT
                else:
                    g = ch - 8
                    rhs = YT.ap[0:64, g * 2304 + 111 + a:g * 2304 + 111 + b]
                    rt = YT
                kb.op("tensor", (lambda e, ps=ps, ch=ch, rhs=rhs, nchk=nchk, n=n: e.matmul(
                    ps.ap[:, 0:n], lhsT=wo.ap[0:64, ch * 1024 + nchk * 128:ch * 1024 + nchk * 128 + 128], rhs=rhs,
                    start=(ch == 0), stop=(ch == 15))), [wo, rt], [ps], sig=(ch == 15))
            kb.op("vector", (lambda e, ps=ps, nchk=nchk, a=a, b=b, n=n: e.scalar_tensor_tensor(
                out=xt_cols(nchk, a, b), in0=ps.ap[:, 0:n], scalar=modv(0, 2, nchk), in1=xt_cols(nchk, a, b),
                op0=ALU.mult, op1=ALU.add)), [ps, mod, XT[nchk][ti]], [XT[nchk][ti]])
    mem.release(mo_)
    mem.release_top()
    if debug == "s7":
        o = dbg_out("xT", [128, 8 * NT])
        kb.dma("sync", o.ap(), xT_ap, reads=ALLXT, writes=[t_out])
        kb.emit_all(taps())
        return

    def mask_halo(ap_fn, tiles_fn, nchunks):
        for c in range(nchunks):
            kb.op("vector", (lambda e, c=c: e.tensor_scalar(out=ap_fn(c, 0, HALO), in0=ap_fn(c, 0, HALO),
                                                          scalar1=masks.ap[:, 0:1], scalar2=None, op0=ALU.mult)),
                  [masks] + tiles_fn(c, 0), tiles_fn(c, 0))
            kb.op("vector", (lambda e, c=c: e.tensor_scalar(out=ap_fn(c, NT - HALO, NT), in0=ap_fn(c, NT - HALO, NT),
                                                          scalar1=masks.ap[:, 1:2], scalar2=None, op0=ALU.mult)),
                  [masks] + tiles_fn(c, 4), tiles_fn(c, 4))

    def ffn(l):
        mm0 = mem.mark()
        hT_ap = mem.bf16(8 * NT)
        HT = [[Tile(hT_ap[:, c * NT + a:c * NT + b]) for (a, b) in TT] for c in range(8)]

        def hcols(c, a, b):
            return hT_ap[:, c * NT + a:c * NT + b]
        norm_to_hT(xt_cols, tt_list(XT), lambda c: derv(l, 2, c), lambda c: modv(l, 3, c, 0), HT, hcols)
        mask_halo(hcols, lambda c, ti: [HT[c][ti]], 8)
        gT = Tile(mem.bf16(6 * NT))
        wd2 = [Tile(mem.bf16(6 * 1024)) for _ in range(2)]
        wu = [Tile(mem.bf16(8 * 256)) for _ in range(2)]
        wus = [Tile(mem.f32(8 * 256)) for _ in range(2)]
        va_ = [Tile(mem.f32(420)) for _ in range(2)]
        vb_ = [Tile(mem.f32(420)) for _ in range(2)]
        cnt = [0, 0]
        groups = [(0, 6), (6, 12), (12, 17), (17, 22)]

        def load_wu(j):
            st = wus[j % 2]
            w_ = wu[j % 2]
            st3 = st.ap.rearrange("p (kc n) -> p kc n", kc=8)
            kb.dma("sync", st3[:, :, 0:128],
                   wup_d.ap()[l, :, j * 128:(j + 1) * 128].rearrange("(kc p) n -> p kc n", p=128), writes=[st])
            kb.dma("sync", st3[:, :, 128:256],
                   wup_d.ap()[l, :, FFN + j * 128:FFN + (j + 1) * 128].rearrange("(kc p) n -> p kc n", p=128),
                   writes=[st])
            kb.op("gpsimd", (lambda e, st=st, w_=w_: e.tensor_copy(out=w_.ap, in_=st.ap)), [st], [w_])

        def load_wd(gi):
            g0, g1 = groups[gi]
            wdt = wd2[gi % 2]
            for jj in range(g0, g1):
                kb.dma("gpsimd", wdt.ap[:, (jj - g0) * 1024:(jj - g0 + 1) * 1024],
                       wdown_d.ap()[l, jj * 128:(jj + 1) * 128, :], writes=[wdt])
        load_wu(0)
        load_wd(0)
        for gi, (j0, j1) in enumerate(groups):
            wd = wd2[gi % 2]
            for j in range(j0, j1):
                w_ = wu[j % 2]
                if j + 1 < NJ:
                    load_wu(j + 1)
                if j == j0 + 1 and gi + 1 < len(groups):
                    load_wd(gi + 1)
                for ti, (a, b) in enumerate(TT):
                    n = b - a
                    a2, b2 = max(a - 1, 0), min(b + 1, NT)
                    n2 = b2 - a2
                    o = a - a2
                    rtl = sorted(set([ti] + ([ti - 1] if ti > 0 else []) + ([ti + 1] if ti < 4 else [])))
                    rt = [HT[kc][t_] for kc in range(8) for t_ in rtl]
                    vv = []
                    for half in range(2):
                        ps = nextps()
                        for kc in range(8):
                            kb.op("tensor", (lambda e, ps=ps, kc=kc, half=half, w_=w_, a2=a2, b2=b2, n2=n2: e.matmul(
                                ps.ap[:, 0:n2], lhsT=w_.ap[:, kc * 256 + half * 128:kc * 256 + half * 128 + 128],
                                rhs=hT_ap[:, kc * NT + a2:kc * NT + b2], start=(kc == 0), stop=(kc == 7))),
                                [w_] + rt, [ps], sig=(kc == 7))
                        chn = j if half == 0 else NJ + j
                        v = (va_ if half == 0 else vb_)[cnt[0] % 2]
                        vv.append(v)
                        wt = [vcol(V_WFDW + (l * 3 + k) * 44, chn) for k in range(3)]
                        bcol = vcol(V_BFDW + l * 44, chn)
                        kb.op("scalar", (lambda e, ps=ps, v=v, o=o, n=n, wt=wt, bcol=bcol: e.activation(
                            out=v.ap[:, 0:n], in_=ps.ap[:, o:o + n], func=AF.Identity, scale=wt[1], bias=bcol)),
                            [ps, vecs], [v])
                        lo = 1 if a == 0 else 0
                        kb.op("vector", (lambda e, ps=ps, v=v, o=o, n=n, wt=wt, lo=lo: e.scalar_tensor_tensor(
                            out=v.ap[:, lo:n], in0=ps.ap[:, o - 1 + lo:o - 1 + n], scalar=wt[0], in1=v.ap[:, lo:n],
                            op0=ALU.mult, op1=ALU.add)), [ps, vecs, v], [v])
                        hi_ = n - 1 if b == NT else n
                        kb.op("vector", (lambda e, ps=ps, v=v, o=o, hi_=hi_, wt=wt: e.scalar_tensor_tensor(
                            out=v.ap[:, 0:hi_], in0=ps.ap[:, o + 1:o + 1 + hi_], scalar=wt[2], in1=v.ap[:, 0:hi_],
                            op0=ALU.mult, op1=ALU.add)), [ps, vecs, v], [v])
                    cnt[0] += 1
                    v_a, v_b = vv
                    kb.op("scalar", (lambda e, v_a=v_a, n=n: e.activation(out=v_a.ap[:, 0:n], in_=v_a.ap[:, 0:n],
                                                                        func=AF.Silu)), [v_a], [v_a])
                    kb.op("gpsimd", (lambda e, v_a=v_a, v_b=v_b, n=n, j=j, j0=j0, a=a, b=b: e.tensor_tensor(
                        out=gT.ap[:, (j - j0) * NT + a:(j - j0) * NT + b], in0=v_a.ap[:, 0:n], in1=v_b.ap[:, 0:n],
                        op=ALU.mult)), [v_a, v_b], [gT])
            for nchk in range(8):
                for ti, (a, b) in enumerate(TT):
                    n = b - a
                    ps = nextps()
                    for j in range(j0, j1):
                        kb.op("tensor", (lambda e, ps=ps, j=j, j0=j0, nchk=nchk, a=a, b=b, n=n, wd=wd: e.matmul(
                            ps.ap[:, 0:n], lhsT=wd.ap[:, (j - j0) * 1024 + nchk * 128:(j - j0) * 1024 + nchk * 128 + 128],
                            rhs=gT.ap[:, (j - j0) * NT + a:(j - j0) * NT + b], start=(j == j0), stop=(j == j1 - 1))),
                            [wd, gT], [ps], sig=(j == j1 - 1))
                    kb.op("vector", (lambda e, ps=ps, nchk=nchk, a=a, b=b, n=n: e.scalar_tensor_tensor(
                        out=xt_cols(nchk, a, b), in0=ps.ap[:, 0:n], scalar=modv(l, 5, nchk), in1=xt_cols(nchk, a, b),
                        op0=ALU.mult, op1=ALU.add)), [ps, mod, XT[nchk][ti]], [XT[nchk][ti]])
        mem.release(mm0)

    ffn(0)
    if debug == "s8":
        o = dbg_out("xT", [128, 8 * NT])
        kb.dma("sync", o.ap(), xT_ap, reads=ALLXT, writes=[t_out])
        kb.emit_all(taps())
        return

    mc = mem.mark()
    h1_ap = mem.bf16(8 * NT)
    H1 = [[Tile(h1_ap[:, c * NT + a:c * NT + b]) for (a, b) in TT] for c in range(8)]

    def hcols1(c, a, b):
        return h1_ap[:, c * NT + a:c * NT + b]
    norm_to_hT(xt_cols, tt_list(XT), lambda c: derv(1, 0, c), lambda c: modv(1, 0, c, 0), H1, hcols1)
    UW = NT + 30
    uT = Tile(mem.bf16(8 * UW))
    kb.op("gpsimd", lambda e: e.memset(uT.ap, 0.0), [], [uT])
    ssum = Tile(mem.f32(NT))
    ssq = Tile(mem.f32(NT))
    msq = Tile(mem.f32(NT))
    mcA = mem.mark()
    wu = [Tile(mem.bf16(8 * 256)) for _ in range(2)]
    wus = [Tile(mem.f32(8 * 256)) for _ in range(2)]
    sg = [Tile(mem.f32(418)) for _ in range(2)]
    cnt = 0
    def load_w1(c):
        st = wus[c % 2]
        w_ = wu[c % 2]
        st3 = st.ap.rearrange("p (kc n) -> p kc n", kc=8)
        kb.dma("sync", st3[:, :, 0:128], wpw1_d.ap()[:, c * 128:(c + 1) * 128].rearrange("(kc p) n -> p kc n", p=128),
               writes=[st])
        kb.dma("sync", st3[:, :, 128:256],
               wpw1_d.ap()[:, D + c * 128:D + (c + 1) * 128].rearrange("(kc p) n -> p kc n", p=128), writes=[st])
        kb.op("gpsimd", (lambda e, st=st, w_=w_: e.tensor_copy(out=w_.ap, in_=st.ap)), [st], [w_])
    load_w1(0)
    for c in range(8):
        w_ = wu[c % 2]
        if c + 1 < 8:
            load_w1(c + 1)
        for ti, (a, b) in enumerate(TT):
            n = b - a
            pp = []
            for half in range(2):
                ps = nextps()
                pp.append(ps)
                for kc in range(8):
                    kb.op("tensor", (lambda e, ps=ps, kc=kc, half=half, w_=w_, a=a, b=b, n=n: e.matmul(
                        ps.ap[:, 0:n], lhsT=w_.ap[:, kc * 256 + half * 128:kc * 256 + half * 128 + 128],
                        rhs=h1_ap[:, kc * NT + a:kc * NT + b], start=(kc == 0), stop=(kc == 7))),
                        [w_] + [H1[kc][ti] for kc in range(8)], [ps], sig=(kc == 7))
            s_ = sg[cnt % 2]
            cnt += 1
            kb.op("scalar", (lambda e, s_=s_, ps=pp[1], c=c, n=n: e.activation(
                out=s_.ap[:, 0:n], in_=ps.ap[:, 0:n], func=AF.Sigmoid, bias=vcol(V_BPW1 + 8, c))), [pp[1], vecs], [s_])
            kb.op("vector", (lambda e, s_=s_, ps=pp[0], c=c, a=a, b=b, n=n: e.scalar_tensor_tensor(
                out=uT.ap[:, c * UW + 15 + a:c * UW + 15 + b], in0=ps.ap[:, 0:n], scalar=vcol(V_BPW1, c),
                in1=s_.ap[:, 0:n], op0=ALU.add, op1=ALU.mult)), [pp[0], s_, vecs], [uT])
    mask_halo(lambda c, a, b: uT.ap[:, c * UW + 15 + a:c * UW + 15 + b], lambda c, ti: [uT], 8)
    if debug == "s9a":
        o = dbg_out("uT", [128, 8 * UW], BF16)
        kb.dma("sync", o.ap(), uT.ap, reads=[uT], writes=[t_out])
        kb.emit_all(taps())
        return
    mem.release(mcA)
    vT_ap = h1_ap
    VT = H1
    dg_ap = [mem.bf16(31 * 128) for _ in range(2)]
    dg = [[Tile(dg_ap[bi][:, k * 128:(k + 1) * 128]) for k in range(31)] for bi in range(2)]

    def build_diag(c):
        for k in range(31):
            dk = dg[c % 2][k]
            kb.op("vector", (lambda e, dk=dk, k=k, c=c: e.tensor_scalar(
                out=dk.ap, in0=ident, scalar1=vcol(V_WCDW + 8 * k, c), scalar2=None,
                op0=ALU.mult)), [consts, vecs], [dk])
    vtmp = [Tile(mem.f32(418)) for _ in range(2)]
    sqb = [Tile(mem.bf16(418)) for _ in range(2)]
    cnt = 0
    build_diag(0)
    for c in range(8):
        d_ = dg[c % 2]
        if c + 1 < 8:
            build_diag(c + 1)
        for ti, (a, b) in enumerate(TT):
            n = b - a
            ps = nextps()
            for k in range(31):
                kb.op("tensor", (lambda e, ps=ps, d_=d_, k=k, c=c, a=a, n=n: e.matmul(
                    ps.ap[:, 0:n], lhsT=d_[k].ap, rhs=uT.ap[:, c * UW + a + k:c * UW + a + k + n],
                    start=(k == 0), stop=(k == 30))), [d_[k], uT], [ps], sig=(k == 30))
            vt = vtmp[cnt % 2]
            sb = sqb[cnt % 2]
            cnt += 1
            kb.op("scalar", (lambda e, ps=ps, vt=vt, c=c, n=n: e.activation(
                out=vt.ap[:, 0:n], in_=ps.ap[:, 0:n], func=AF.Identity, bias=vcol(V_BCDW, c))), [ps, vecs], [vt])
            kb.op("vector", (lambda e, vt=vt, c=c, a=a, b=b, n=n: e.tensor_copy(
                out=vT_ap[:, c * NT + a:c * NT + b], in_=vt.ap[:, 0:n])), [vt], [VT[c][ti]])
            kb.op("scalar", (lambda e, vt=vt, sb=sb, n=n: e.activation(out=sb.ap[:, 0:n], in_=vt.ap[:, 0:n],
                                                                      func=AF.Square)), [vt], [sb])
            ps1 = nextps()
            kb.op("tensor", (lambda e, ps1=ps1, c=c, a=a, b=b, n=n: e.matmul(
                ps1.ap[:, 0:n], lhsT=ones_bf, rhs=vT_ap[:, c * NT + a:c * NT + b], start=True, stop=True)),
                [cbf, VT[c][ti]], [ps1])
            ps2 = nextps()
            kb.op("tensor", (lambda e, ps2=ps2, sb=sb, n=n: e.matmul(
                ps2.ap[:, 0:n], lhsT=ones_bf, rhs=sb.ap[:, 0:n], start=True, stop=True)), [cbf, sb], [ps2])
            if c == 0:
                kb.op("vector", (lambda e, ps1=ps1, a=a, b=b, n=n: e.tensor_copy(out=ssum.ap[:, a:b], in_=ps1.ap[:, 0:n])),
                      [ps1], [ssum])
                kb.op("vector", (lambda e, ps2=ps2, a=a, b=b, n=n: e.tensor_copy(out=ssq.ap[:, a:b], in_=ps2.ap[:, 0:n])),
                      [ps2], [ssq])
            else:
                kb.op("vector", (lambda e, ps1=ps1, a=a, b=b, n=n: e.tensor_tensor(
                    out=ssum.ap[:, a:b], in0=ps1.ap[:, 0:n], in1=ssum.ap[:, a:b], op=ALU.add)), [ps1, ssum], [ssum])
                kb.op("vector", (lambda e, ps2=ps2, a=a, b=b, n=n: e.tensor_tensor(
                    out=ssq.ap[:, a:b], in0=ps2.ap[:, 0:n], in1=ssq.ap[:, a:b], op=ALU.add)), [ps2, ssq], [ssq])
    if debug == "s9b":
        o = dbg_out("vT", [128, 8 * NT], BF16)
        kb.dma("sync", o.ap(), vT_ap, reads=[t for row in VT for t in row], writes=[t_out])
        o = dbg_out("ssum", [128, NT])
        kb.dma("sync", o.ap(), ssum.ap, reads=[ssum], writes=[t_out])
        kb.emit_all(taps())
        return
    kb.op("vector", lambda e: e.tensor_scalar(out=ssum.ap, in0=ssum.ap, scalar1=1.0 / D, scalar2=None, op0=ALU.mult),
          [ssum], [ssum])
    mem.release(mcA)
    kb.op("vector", lambda e: e.tensor_tensor(out=msq.ap, in0=ssum.ap, in1=ssum.ap, op=ALU.mult), [ssum], [msq])
    kb.op("vector", lambda e: e.scalar_tensor_tensor(out=ssq.ap, in0=ssq.ap, scalar=1.0 / D, in1=msq.ap,
                                                     op0=ALU.mult, op1=ALU.subtract), [ssq, msq], [ssq])
    kb.op("scalar", lambda e: e.activation(out=ssq.ap, in_=ssq.ap, func=AF.Sqrt, bias=epsb.ap[:, 1:2]), [ssq, epsb], [ssq])
    kb.op("vector", lambda e: e.reciprocal(out=ssq.ap, in_=ssq.ap), [ssq], [ssq])
    tmpn = [Tile(mem.f32(418)) for _ in range(2)]
    cnt = 0
    for c in range(8):
        for ti, (a, b) in enumerate(TT):
            n = b - a
            t = tmpn[cnt % 2]
            cnt += 1
            kb.op("vector", (lambda e, t=t, c=c, a=a, b=b, n=n: e.tensor_tensor(
                out=t.ap[:, 0:n], in0=vT_ap[:, c * NT + a:c * NT + b], in1=ssum.ap[:, a:b], op=ALU.subtract)),
                [VT[c][ti], ssum], [t])
            kb.op("gpsimd", (lambda e, t=t, a=a, b=b, n=n: e.tensor_tensor(
                out=t.ap[:, 0:n], in0=t.ap[:, 0:n], in1=ssq.ap[:, a:b], op=ALU.mult)), [t, ssq], [t])
            kb.op("scalar", (lambda e, t=t, c=c, a=a, b=b, n=n: e.activation(
                out=vT_ap[:, c * NT + a:c * NT + b], in_=t.ap[:, 0:n], func=AF.Silu, scale=vcol(V_LNG, c),
                bias=vcol(V_LNB, c))), [t, vecs], [VT[c][ti]])
    if debug == "s9c":
        o = dbg_out("vT", [128, 8 * NT], BF16)
        kb.dma("sync", o.ap(), vT_ap, reads=[t for row in VT for t in row], writes=[t_out])
        kb.emit_all(taps())
        return
    w2 = Tile(mem.bf16(8 * 1024))
    for kc in range(8):
        kb.dma("gpsimd", w2.ap[:, kc * 1024:(kc + 1) * 1024], wpw2_d.ap()[kc * 128:(kc + 1) * 128, :], writes=[w2])
    cnt = 0
    for nchk in range(8):
        for ti, (a, b) in enumerate(TT):
            n = b - a
            ps = nextps()
            for kc in range(8):
                kb.op("tensor", (lambda e, ps=ps, kc=kc, nchk=nchk, a=a, b=b, n=n: e.matmul(
                    ps.ap[:, 0:n], lhsT=w2.ap[:, kc * 1024 + nchk * 128:kc * 1024 + nchk * 128 + 128],
                    rhs=vT_ap[:, kc * NT + a:kc * NT + b], start=(kc == 0), stop=(kc == 7))),
                    [w2, VT[kc][ti]], [ps], sig=(kc == 7))
            t = tmpn[cnt % 2]
            cnt += 1
            kb.op("scalar", (lambda e, ps=ps, t=t, nchk=nchk, n=n: e.activation(
                out=t.ap[:, 0:n], in_=ps.ap[:, 0:n], func=AF.Identity, scale=modv(1, 2, nchk),
                bias=der.ap[:, 48 + nchk:49 + nchk])), [ps, mod, der], [t])
            kb.op("vector", (lambda e, t=t, nchk=nchk, a=a, b=b, n=n: e.tensor_tensor(
                out=xt_cols(nchk, a, b), in0=t.ap[:, 0:n], in1=xt_cols(nchk, a, b), op=ALU.add)),
                [t, XT[nchk][ti]], [XT[nchk][ti]])
    mem.release(mc)
    if debug == "s10":
        o = dbg_out("xT", [128, 8 * NT])
        kb.dma("sync", o.ap(), xT_ap, reads=ALLXT, writes=[t_out])
        kb.emit_all(taps())
        return

    ffn(1)

    ost = [Tile(mem.f32(D)) for _ in range(2)]
    for t in range(16):
        c0 = HALO + 128 * t
        o_ = ost[t % 2]
        for hlf in range(2):
            ps = nextps()
            for cc in range(4):
                c = hlf * 4 + cc
                kb.op("tensor", (lambda e, ps=ps, cc=cc, c=c, c0=c0: e.transpose(
                    out=ps.ap[:, cc * 128:(cc + 1) * 128], in_=xT_ap[:, c * NT + c0:c * NT + c0 + 128],
                    identity=ident)), [consts] + XT[c], [ps], sig=(cc == 3))
            if hlf == 0:
                kb.op("scalar", (lambda e, ps=ps, o_=o_: e.activation(out=o_.ap[:, 0:512], in_=ps.ap[:, 0:512],
                                                                      func=AF.Identity)), [ps], [o_])
            else:
                kb.op("vector", (lambda e, ps=ps, o_=o_: e.tensor_copy(out=o_.ap[:, 512:1024], in_=ps.ap[:, 0:512])),
                      [ps], [o_])
        kb.dma("gpsimd", out_d.ap()[t * 128:(t + 1) * 128, :], o_.ap, reads=[o_], writes=[Tile(None)])
    kb.emit_all(taps())


def _vec8(v):
    v = np.asarray(v, np.float32).reshape(-1, 128)
    return np.ascontiguousarray(v.T)


def _prep_shared(inp):
    sh = {}
    vecs = np.zeros((128, NVEC), np.float32)
    for l in range(2):
        vecs[:, V_GMIX + 8 * l:V_GMIX + 8 * l + 8] = _vec8(inp["g_mix"][l])
        vecs[:, V_GFFN + 8 * l:V_GFFN + 8 * l + 8] = _vec8(inp["g_ffn"][l])
        vecs[:, V_BADA + 48 * l:V_BADA + 48 * l + 48] = _vec8(inp["b_ada"][l])
        vecs[:, V_BFDW + 44 * l:V_BFDW + 44 * l + 44] = _vec8(inp["b_fdw"][l])
        for k in range(3):
            o = V_WFDW + (l * 3 + k) * 44
            vecs[:, o:o + 44] = _vec8(inp["w_fdw"][l, k])
    vecs[:, V_BPW1:V_BPW1 + 16] = _vec8(inp["b_pw1"][0])
    vecs[:, V_BCDW:V_BCDW + 8] = _vec8(inp["b_cdw"][0])
    vecs[:, V_LNG:V_LNG + 8] = _vec8(inp["ln_g"][0])
    vecs[:, V_LNB:V_LNB + 8] = _vec8(inp["ln_b"][0])
    vecs[:, V_BPW2:V_BPW2 + 8] = _vec8(inp["b_pw2"][0])
    for k in range(31):
        vecs[:, V_WCDW + 8 * k:V_WCDW + 8 * k + 8] = _vec8(inp["w_cdw"][0, k])
    vecs[:, V_QK] = np.tile(np.asarray(inp["q_gain"][0], np.float32), 2)
    vecs[:, V_QK + 1] = np.tile(np.asarray(inp["k_gain"][0], np.float32), 2)
    sh["vecs"] = vecs
    consts = np.zeros((128, 640), np.float32)
    consts[64, 512:576] = 1.0
    consts[:, 0:128] = np.eye(128, dtype=np.float32)
    consts[:, 128:256] = 1.0
    for hh in range(2):
        consts[hh * 64:(hh + 1) * 64, 256 + hh * 64:256 + (hh + 1) * 64] = 1.0
    for dst in range(128):
        within = dst % 32
        partner = dst + 16 if within < 16 else dst - 16
        consts[partner, 384 + dst] = 1.0
    sh["consts"] = consts
    cc = np.stack([np.asarray(inp["c"][0], np.float32), np.asarray(inp["c_ctx"], np.float32)], 0)
    sh["cT"] = np.ascontiguousarray(cc.reshape(2, 8, 128).transpose(2, 1, 0).reshape(128, 16))
    sh["ctx"] = np.ascontiguousarray(inp["ctx"][0], dtype=np.float32)
    sh["w_in"] = np.ascontiguousarray(inp["w_in_hyb"][0], dtype=np.float32)
    sh["w_out"] = np.ascontiguousarray(inp["w_out_hyb"][0], dtype=np.float32)
    sh["w_pw1"] = np.ascontiguousarray(inp["w_pw1"][0], dtype=np.float32)
    sh["w_pw2"] = np.ascontiguousarray(inp["w_pw2"][0], dtype=np.float32)
    sh["w_up"] = np.ascontiguousarray(inp["w_up"], dtype=np.float32)
    sh["w_down"] = np.ascontiguousarray(inp["w_down"], dtype=np.float32)
    bf = ml_dtypes.bfloat16
    cidx = np.arange(64)[:, None].astype(np.float64)
    midx = np.arange(64)[None, :].astype(np.float64)
    ang = 2 * np.pi * cidx * midx / 64.0
    sh["e64"] = np.concatenate([np.cos(ang), -np.sin(ang)], 1).astype(bf)
    a = np.arange(128)[:, None].astype(np.float64)
    b = np.arange(128)[None, :].astype(np.float64)
    ang = 2 * np.pi * a * b / 128.0
    C, S = np.cos(ang), np.sin(ang)
    sh["cs12"] = np.concatenate([C, -S, S, C], 1).astype(bf)
    ang = 2 * np.pi * a * b / 16384.0
    sh["tw"] = np.concatenate([np.tile(np.cos(ang), (1, 4)), np.tile(np.sin(ang), (1, 4))], 1).astype(np.float32)
    return sh


def _prep_core(inp, i):
    pc = {}
    x = inp["x"][0]
    xi = np.zeros((NTP, D), np.float32)
    g0 = OWN * i - HALO
    lo, hi = max(g0, 0), min(g0 + NT, SEQ)
    xi[lo - g0:hi - g0] = x[lo:hi]
    pc["x"] = xi
    pos = np.clip(np.arange(g0, g0 + NT), 0, SEQ - 1)
    row = (pos // 64).astype(np.float32)
    col = (pos % 64).astype(np.float32)
    inv_freq = (1.0 / (np.float32(10000.0) ** (np.arange(0, 32, 2, dtype=np.float32) / np.float32(32)))).astype(np.float32)
    cos = np.zeros((64, NT), np.float32)
    sin = np.zeros((64, NT), np.float32)
    for d in range(64):
        p = row if d < 32 else col
        within = d % 32
        f = within % 16
        angv = (p * inv_freq[f]).astype(np.float32)
        cos[d] = np.cos(angv)
        sin[d] = np.sin(angv) * (-1.0 if within < 16 else 1.0)
    pc["rope"] = np.concatenate([np.tile(cos, (2, 1)), np.tile(sin, (2, 1))], 1).astype(np.float32)
    m = np.ones((128, 2), np.float32)
    if i == 0:
        m[:, 0] = 0.0
    if i == NCORE - 1:
        m[:, 1] = 0.0
    pc["masks"] = m
    pc["w_ada"] = np.ascontiguousarray(inp["w_ada"][:, :, 768 * i:768 * (i + 1)], dtype=np.float32)
    pc["wf"] = np.ascontiguousarray(inp["w_in_hyb"][0][:, 768 + 64 * i:768 + 64 * (i + 1)], dtype=np.float32)
    l1 = np.arange(128)[:, None].astype(np.float64)
    k1 = ((16 * i - 1 + np.arange(18)) % 128)[None, :].astype(np.float64)
    ang = 2 * np.pi * l1 * k1 / 128.0
    pc["cs3"] = np.concatenate([np.cos(ang), np.sin(ang)], 1).astype(ml_dtypes.bfloat16)
    return pc


_NC_CACHE = {}
_LAST = {}


def kernel(**inputs):
    inp = {k: np.asarray(v) for k, v in inputs.items()}
    if "nc" not in _NC_CACHE:
        _NC_CACHE["nc"] = build_program()
    nc = _NC_CACHE["nc"]
    sh = _prep_shared(inp)
    in_maps = []
    for i in range(NCORE):
        m = dict(sh)
        m.update(_prep_core(inp, i))
        in_maps.append(m)
    res = run_bass_kernel_spmd(nc, in_maps, core_ids=list(range(NCORE)))
    out = np.concatenate([np.asarray(r["out"], np.float32) for r in res.results], 0)
    return out.reshape(1, SEQ, D)
```
